# Optimizing a Trainium2 kernel written in Bass

```python
import math
import jax, jax.numpy as jnp
from jax import lax
import numpy as np

D_MODEL = 2048
BATCH = 4
SEQ = 4096
DEPTH = 1

MIX_WIDTH = D_MODEL
ATTN_WIDTH = MIX_WIDTH // 2
CONV_WIDTH = MIX_WIDTH - ATTN_WIDTH
DIFF_HEAD_DIM = 64
DIFF_V_DIM = 2 * DIFF_HEAD_DIM
N_DIFF_HEADS = ATTN_WIDTH // DIFF_V_DIM
IN_COLS = 3 * ATTN_WIDTH + 2 * CONV_WIDTH
CONV_KERNEL = 31
D_FF = 5632
ROPE_THETA = 10000.0
Q_BLOCK = 128
RMS_EPS = 1e-6
LN_EPS = 1e-5
FFN_RESIDUAL_WEIGHT = 0.5
N_MOD = 9
POS_OFFSET_MAX = 1024

kernel_name = 'hybrid_diffattn_conformer_macaron_block'


def lambda_init(layer_idx):
    return 0.8 - 0.6 * math.exp(-0.3 * layer_idx)


def rms_norm(x, g, eps=RMS_EPS):
    xf = x.astype(jnp.float32)
    y = xf * lax.rsqrt(jnp.mean(xf * xf, axis=-1, keepdims=True) + eps)
    return (y * g.astype(jnp.float32)).astype(x.dtype)


def layer_norm(x, g, b, eps=LN_EPS):
    xf = x.astype(jnp.float32)
    mu = jnp.mean(xf, axis=-1, keepdims=True)
    var = jnp.mean(jnp.square(xf - mu), axis=-1, keepdims=True)
    y = (xf - mu) * lax.rsqrt(var + eps)
    return (y * g.astype(jnp.float32) + b.astype(jnp.float32)).astype(x.dtype)


def modulate(h, shift, scale):
    return h * (1 + scale[:, None, :]) + shift[:, None, :]


def swiglu(h, w_gu, w_down):
    gu = jnp.einsum('bsd,df->bsf', h, w_gu)
    g, u = jnp.split(gu, 2, axis=-1)
    return jnp.einsum('bsf,fd->bsd', jax.nn.silu(g) * u, w_down)


def rope_tables(positions):
    inv_freq = ROPE_THETA ** (-jnp.arange(0, DIFF_HEAD_DIM, 2, dtype=jnp.float32) / DIFF_HEAD_DIM)
    ang = positions.astype(jnp.float32)[..., None] * inv_freq
    return jnp.cos(ang), jnp.sin(ang)


def apply_rope(t, cos, sin):
    cos = cos[:, :, None, None, :]
    sin = sin[:, :, None, None, :]
    tf = t.astype(jnp.float32)
    t1, t2 = jnp.split(tf, 2, axis=-1)
    out = jnp.concatenate([t1 * cos - t2 * sin, t2 * cos + t1 * sin], axis=-1)
    return out.astype(t.dtype)


def diff_attention(q, k, v, lam):
    B, S, H, _, dk = q.shape
    dv = v.shape[-1]
    nb = S // Q_BLOCK
    scale = dk ** -0.5
    qb = q.reshape(B, nb, Q_BLOCK, H, 2, dk).transpose(1, 0, 2, 3, 4, 5)
    starts = jnp.arange(nb, dtype=jnp.int32) * Q_BLOCK
    key_idx = jnp.arange(S, dtype=jnp.int32)

    def block(args):
        qi, start = args
        s = jnp.einsum('bqhmd,bkhmd->bhmqk', qi, k).astype(jnp.float32) * scale
        q_idx = start + jnp.arange(Q_BLOCK, dtype=jnp.int32)
        causal = key_idx[None, :] <= q_idx[:, None]
        p = jax.nn.softmax(jnp.where(causal, s, -jnp.inf), axis=-1)
        a = p[:, :, 0] - lam * p[:, :, 1]
        return jnp.einsum('bhqk,bkhd->bqhd', a.astype(v.dtype), v)

    out = lax.map(block, (qb, starts))
    return out.transpose(1, 0, 2, 3, 4).reshape(B, S, H, dv)


def causal_depthwise_conv(u, w, b):
    K, C = w.shape
    y = lax.conv_general_dilated(u, w[:, None, :].astype(u.dtype), window_strides=(1,),
                                 padding=[(K - 1, 0)], dimension_numbers=('NWC', 'WIO', 'NWC'),
                                 feature_group_count=C)
    return y + b


def hybrid_mixer(h, cos, sin, w_in, q_norm, k_norm, lambda_q1, lambda_k1, lambda_q2, lambda_k2,
                 subln, conv_w, conv_b, conv_ln_g, conv_ln_b, w_out, lam_init):
    B, S, _ = h.shape
    proj = jnp.einsum('bsd,de->bse', h, w_in)
    q, k, v, conv_a, conv_g = jnp.split(
        proj, [ATTN_WIDTH, 2 * ATTN_WIDTH, 3 * ATTN_WIDTH, 3 * ATTN_WIDTH + CONV_WIDTH], axis=-1)
    q = q.reshape(B, S, N_DIFF_HEADS, 2, DIFF_HEAD_DIM)
    k = k.reshape(B, S, N_DIFF_HEADS, 2, DIFF_HEAD_DIM)
    v = v.reshape(B, S, N_DIFF_HEADS, DIFF_V_DIM)
    q = apply_rope(rms_norm(q, q_norm), cos, sin)
    k = apply_rope(rms_norm(k, k_norm), cos, sin)
    lam = (jnp.exp(jnp.sum(lambda_q1.astype(jnp.float32) * lambda_k1.astype(jnp.float32)))
           - jnp.exp(jnp.sum(lambda_q2.astype(jnp.float32) * lambda_k2.astype(jnp.float32)))
           + lam_init)
    attn = diff_attention(q, k, v, lam)
    attn = (rms_norm(attn, subln) * (1.0 - lam_init)).reshape(B, S, ATTN_WIDTH)
    u = conv_a * jax.nn.sigmoid(conv_g)
    u = causal_depthwise_conv(u, conv_w, conv_b)
    u = jax.nn.silu(layer_norm(u, conv_ln_g, conv_ln_b))
    merged = jnp.concatenate([attn, u], axis=-1)
    return jnp.einsum('bse,ed->bsd', merged, w_out)


def setup_inputs(seed: int = 0) -> dict:
    key = jax.random.key(seed)
    ks = jax.random.split(key, 32)
    f32 = jnp.float32
    D, F, L = D_MODEL, D_FF, DEPTH

    def nrm(k, shape, s):
        return jax.random.normal(k, shape, f32) * s

    gate_offset = jnp.repeat(jnp.tile(jnp.array([0.0, 0.0, 1.0], f32), 3), D)
    positions = (jnp.arange(SEQ, dtype=jnp.int32)[None, :]
                 + jax.random.randint(ks[2], (BATCH, 1), 0, POS_OFFSET_MAX, dtype=jnp.int32))
    return {
        'x': nrm(ks[0], (BATCH, SEQ, D), 1.0),
        'c': nrm(ks[1], (BATCH, D), 1.0),
        'positions': positions,
        'w_ada': nrm(ks[3], (L, D, N_MOD * D), 0.1 * D ** -0.5),
        'b_ada': nrm(ks[4], (L, N_MOD * D), 0.02) + gate_offset,
        'ffn1_norm': 1.0 + nrm(ks[5], (L, D), 0.02),
        'ffn1_w_gu': nrm(ks[6], (L, D, 2 * F), D ** -0.5),
        'ffn1_w_down': nrm(ks[7], (L, F, D), F ** -0.5),
        'mix_norm': 1.0 + nrm(ks[8], (L, D), 0.02),
        'w_in': nrm(ks[9], (L, D, IN_COLS), D ** -0.5),
        'q_norm': 1.0 + nrm(ks[10], (L, DIFF_HEAD_DIM), 0.02),
        'k_norm': 1.0 + nrm(ks[11], (L, DIFF_HEAD_DIM), 0.02),
        'lambda_q1': nrm(ks[12], (L, DIFF_HEAD_DIM), 0.1),
        'lambda_k1': nrm(ks[13], (L, DIFF_HEAD_DIM), 0.1),
        'lambda_q2': nrm(ks[14], (L, DIFF_HEAD_DIM), 0.1),
        'lambda_k2': nrm(ks[15], (L, DIFF_HEAD_DIM), 0.1),
        'subln': 1.0 + nrm(ks[16], (L, DIFF_V_DIM), 0.02),
        'conv_w': nrm(ks[17], (L, CONV_KERNEL, CONV_WIDTH), CONV_KERNEL ** -0.5),
        'conv_b': nrm(ks[18], (L, CONV_WIDTH), 0.02),
        'conv_ln_g': 1.0 + nrm(ks[19], (L, CONV_WIDTH), 0.02),
        'conv_ln_b': nrm(ks[20], (L, CONV_WIDTH), 0.02),
        'w_out': nrm(ks[21], (L, MIX_WIDTH, D), MIX_WIDTH ** -0.5),
        'ffn2_norm': 1.0 + nrm(ks[22], (L, D), 0.02),
        'ffn2_w_gu': nrm(ks[23], (L, D, 2 * F), D ** -0.5),
        'ffn2_w_down': nrm(ks[24], (L, F, D), F ** -0.5),
    }


def reference(x, c, positions, w_ada, b_ada, ffn1_norm, ffn1_w_gu, ffn1_w_down, mix_norm, w_in,
              q_norm, k_norm, lambda_q1, lambda_k1, lambda_q2, lambda_k2, subln, conv_w, conv_b,
              conv_ln_g, conv_ln_b, w_out, ffn2_norm, ffn2_w_gu, ffn2_w_down):
    cos, sin = rope_tables(positions)
    c_act = jax.nn.silu(c)
    for l in range(DEPTH):
        ada = jnp.einsum('bd,de->be', c_act, w_ada[l]) + b_ada[l]
        sh1, sc1, g1, shm, scm, gm, sh2, sc2, g2 = jnp.split(ada, N_MOD, axis=-1)
        h = modulate(rms_norm(x, ffn1_norm[l]), sh1, sc1)
        x = x + FFN_RESIDUAL_WEIGHT * g1[:, None, :] * swiglu(h, ffn1_w_gu[l], ffn1_w_down[l])
        h = modulate(rms_norm(x, mix_norm[l]), shm, scm)
        y = hybrid_mixer(h, cos, sin, w_in[l], q_norm[l], k_norm[l], lambda_q1[l], lambda_k1[l],
                         lambda_q2[l], lambda_k2[l], subln[l], conv_w[l], conv_b[l],
                         conv_ln_g[l], conv_ln_b[l], w_out[l], lambda_init(l))
        x = x + gm[:, None, :] * y
        h = modulate(rms_norm(x, ffn2_norm[l]), sh2, sc2)
        x = x + FFN_RESIDUAL_WEIGHT * g2[:, None, :] * swiglu(h, ffn2_w_gu[l], ffn2_w_down[l])
    return x
```

```python
import math
import numpy as np
import ml_dtypes
import concourse.bass as bass
import concourse.mybir as mybir
from concourse.bass_utils import run_bass_kernel_spmd

F32 = mybir.dt.float32
BF16 = mybir.dt.bfloat16
I32 = mybir.dt.int32
ALU = mybir.AluOpType
AF = mybir.ActivationFunctionType

D = 2048
FF = 5632
NBLK = 4
TB = 512
NTOK = NBLK * TB
NADA = 9 * D
RMS_EPS = 1e-6
LN_EPS = 1e-5
LAM_INIT = 0.8 - 0.6 * math.exp(-0.3 * 0)
NSLOT = 6
TWO_PI = 2.0 * math.pi
C1 = 6.28125
C2 = TWO_PI - C1

V_N1, V_NM, V_N2 = 0, 16, 32
V_CB, V_LG, V_LB = 48, 56, 64
V_GQ, V_GK, V_SUB, V_INVF, V_SEL0, V_SEL1, V_SSIGN = 72, 73, 74, 75, 76, 77, 78
V_CW = 80
NV = V_CW + 8 * 31


class Buf:
    __slots__ = ("w", "r", "name")

    def __init__(self, name=""):
        self.w = None
        self.r = {}
        self.name = name


class Prog:
    ENG = ("pe", "act", "dve", "pool", "sp")

    def __init__(self, nc):
        self.nc = nc
        self.ops = {e: [] for e in self.ENG}
        self.sem = {e: nc.alloc_semaphore("s_" + e) for e in self.ENG if e != "sp"}
        self.cnt = {e: 0 for e in self.ENG}
        self.seen = {e: {} for e in self.ENG}
        self.dcnt = {}
        self.nsem = 0

    def newsem(self, name=None):
        self.nsem += 1
        s = self.nc.alloc_semaphore(name or ("d%d" % self.nsem))
        self.dcnt[s] = 0
        return s

    def _waits(self, eng, reads, writes):
        need = {}
        seq = len(self.ops[eng])

        def add(tok, raw):
            if tok is None:
                return
            sem, val, teng, tseq = tok
            if teng == eng:
                if eng == "pe" or eng == "sp":
                    return
                if not raw:
                    return
                if seq - tseq > 2:
                    return
            if need.get(sem, 0) < val:
                need[sem] = val

        for b in reads:
            add(b.w, True)
        for b in writes:
            add(b.w, False)
            for t in b.r.values():
                add(t, False)
        out = []
        for sem, val in need.items():
            if self.seen[eng].get(sem, 0) >= val:
                continue
            self.seen[eng][sem] = val
            out.append((sem, val))
        return out

    def op(self, eng, fn, reads=(), writes=(), signal=True):
        waits = self._waits(eng, reads, writes)
        seq = len(self.ops[eng])
        sem = self.sem[eng]
        if signal:
            self.cnt[eng] += 1
            tok = (sem, self.cnt[eng], eng, seq)
            sig = (sem, 1)
        else:
            tok = (sem, self.cnt[eng] + 1, eng, seq)
            sig = None
        self.ops[eng].append((waits, fn, sig))
        for b in reads:
            b.r[eng] = tok
        for b in writes:
            b.w = tok
            b.r = {}
        return tok

    def dma(self, q, out, in_, reads=(), writes=(), sem=None):
        waits = self._waits(q, reads, writes)
        self.dcnt[sem] += 16
        tok = (sem, self.dcnt[sem], "dma", None)
        self.ops[q].append((waits, lambda e, o=out, i=in_: e.dma_start(out=o, in_=i), (sem, 16)))
        for b in reads:
            b.r[sem] = tok
        for b in writes:
            b.w = tok
            b.r = {}
        return tok

    def custom(self, q, fn, reads=(), writes=(), sem=None, inc=1):
        waits = self._waits(q, reads, writes)
        self.dcnt[sem] += inc
        tok = (sem, self.dcnt[sem], "dma", None)
        self.ops[q].append((waits, fn, (sem, inc)))
        for b in reads:
            b.r[sem] = tok
        for b in writes:
            b.w = tok
            b.r = {}
        return tok

    def barrier(self):
        for e in self.ENG:
            waits = []
            for f in ("pe", "act", "dve", "pool"):
                if f != e and self.cnt[f] > self.seen[e].get(self.sem[f], 0):
                    waits.append((self.sem[f], self.cnt[f]))
                    self.seen[e][self.sem[f]] = self.cnt[f]
            for s, v in self.dcnt.items():
                if v > self.seen[e].get(s, 0):
                    waits.append((s, v))
                    self.seen[e][s] = v
            if waits:
                self.ops[e].append((waits, None, None))

    def check_deadlock(self):
        val = {}
        pc = {e: 0 for e in self.ENG}
        n = {e: len(self.ops[e]) for e in self.ENG}
        progress = True
        while progress:
            progress = False
            for e in self.ENG:
                while pc[e] < n[e]:
                    waits, fn, sig = self.ops[e][pc[e]]
                    if any(val.get(s, 0) < v for s, v in waits):
                        break
                    if sig is not None:
                        val[sig[0]] = val.get(sig[0], 0) + sig[1]
                    pc[e] += 1
                    progress = True
        stuck = {e: (pc[e], n[e]) for e in self.ENG if pc[e] < n[e]}
        return stuck

    def emit(self):
        nc = self.nc
        ops = self.ops
        with nc.Block() as block:
            def run(engine, lst):
                for waits, fn, sig in lst:
                    for sem, val in waits:
                        engine.wait_ge(sem, val)
                    if fn is None:
                        continue
                    ins = fn(engine)
                    if sig is not None:
                        ins.then_inc(sig[0], sig[1])

            @block.tensor
            def _(e):
                run(e, ops["pe"])

            @block.scalar
            def _(e):
                run(e, ops["act"])

            @block.vector
            def _(e):
                run(e, ops["dve"])

            @block.gpsimd
            def _(e):
                run(e, ops["pool"])

            @block.sync
            def _(e):
                run(e, ops["sp"])


def build_program(debug_stop=None, shard_w=False, debug_dump=False):
    nc = bass.Bass("TRN2", target_bir_lowering=False)
    P = Prog(nc)
    flags = set((debug_stop or "").split(","))
    if debug_stop is not None:
        debug_stop = "xT"

    def din(name, shape, dt):
        return nc.dram_tensor(name, shape, dt, kind="ExternalInput")

    x_d = din("x", [NTOK, D], F32)
    pos_d = din("pos", [1, NTOK], I32)
    cT_d = din("cT", [128, 16], F32)
    bW = Buf()
    wgather = []

    def dinw(name, shape):
        if shard_w == "none":
            return nc.dram_tensor(name + "_int", shape, F32)
        if not shard_w:
            return din(name, shape, F32)
        sh = din(name, [shape[0] // 8, shape[1]], F32)
        full = nc.dram_tensor(name + "_full", shape, F32)
        wgather.append((sh, full))
        return full

    wada_d = dinw("w_ada", [D, NADA])
    badaT_d = din("b_adaT", [128, 144], F32)
    wgu_d = [dinw("w_gu1", [D, 2 * FF]), dinw("w_gu2", [D, 2 * FF])]
    wdn_d = [dinw("w_down1", [FF, D]), dinw("w_down2", [FF, D])]
    win_d = dinw("w_in", [D, 5120])
    wout_d = dinw("w_out", [D, D])
    vecs_d = din("vecs", [128, NV], F32)
    lam_d = din("lam4", [1, 256], F32)
    masks_d = din("masks", [128, 8 * 512], BF16)
    cb16_d = din("cb16", [128, 4 * 128], BF16)
    identf_d = din("identf", [128, 128], F32)
    out_d = nc.dram_tensor("out", [NTOK, D], F32, kind="ExternalOutput")

    X1_d = nc.dram_tensor("X1", [NBLK * 128, 16 * 512], F32)
    QT_d = nc.dram_tensor("QT", [NBLK * 128, 8 * 2 * 512], BF16)
    UT_d = nc.dram_tensor("UT", [NBLK * 128, 8 * 512], BF16)
    KTo_d = [nc.dram_tensor("KTo%d" % i, [512, NTOK], BF16) for i in range(2)]
    KTa_d = [nc.dram_tensor("KTa%d" % i, [1024, NTOK], BF16) for i in range(2)]
    Vo_d = [nc.dram_tensor("Vo%d" % i, [NTOK // 2, 1024], BF16) for i in range(2)]
    Va_d = [nc.dram_tensor("Va%d" % i, [NTOK, 1024], BF16) for i in range(2)]
    Ho_d = nc.dram_tensor("Ho", [NBLK * 1024, 32], BF16)
    Ha_d = nc.dram_tensor("Ha", [2 * NBLK * 1024, 32], BF16)
    bX1 = [Buf() for _ in range(NBLK)]
    bQT = [Buf() for _ in range(NBLK)]
    bUT = [Buf() for _ in range(NBLK)]
    bKTo, bVo, bHo, bKTa, bVa, bHa = Buf(), Buf(), Buf(), Buf(), Buf(), Buf()

    def sb(name, shape, dt):
        return nc.alloc_sbuf_tensor("sb_" + name, shape, dt)

    xT = sb("xT", [128, 16, 512], F32)
    bxT = [Buf() for _ in range(16)]
    hT = sb("hT", [128, 16, 512], BF16)
    bhT = [Buf() for _ in range(16)]
    actT = sb("actT", [128, 8, 512], BF16)
    bact = [Buf() for _ in range(8)]
    wr = [sb("wr%d" % i, [128, 4096], BF16) for i in range(NSLOT)]
    bwr = [Buf() for _ in range(NSLOT)]
    swr = [P.newsem() for _ in range(NSLOT)]
    xs = [sb("xs%d" % i, [128, 2048], F32) for i in range(2)]
    bxs = [Buf() for _ in range(2)]
    sxs = [P.newsem() for _ in range(2)]
    NTMP = 6
    tmp = [sb("tmp%d" % i, [128, 512], F32) for i in range(NTMP)]
    btmp = [Buf() for _ in range(NTMP)]
    tmpi = [0]

    def newtmp():
        i = tmpi[0] % NTMP
        tmpi[0] += 1
        return tmp[i], btmp[i]

    sqb = [sb("sqb%d" % i, [128, 512], BF16) for i in range(2)]
    bsqb = [Buf() for _ in range(2)]
    sqi = [0]

    def newsq():
        i = sqi[0] % 2
        sqi[0] += 1
        return sqb[i], bsqb[i]

    qbt = [sb("qbt%d" % i, [128, 512], BF16) for i in range(2)]
    bqbt = [Buf() for _ in range(2)]
    qbi = [0]

    def newqb():
        i = qbi[0] % 2
        qbi[0] += 1
        return qbt[i], bqbt[i]

    identf = sb("identf", [128, 128], F32)
    onesf = sb("onesf", [128, 128], F32)
    cb16 = sb("cb16", [128, 4, 128], BF16)
    identb, onesb, blkones, pswap = cb16[:, 0, :], cb16[:, 1, :], cb16[:, 2, :], cb16[:, 3, :]
    vecs = sb("vecs", [128, NV], F32)
    bconst = Buf()
    adaT = sb("adaT", [128, 144], F32)
    badaT_s = sb("badaT", [128, 144], F32)
    mods = sb("mods", [128, 9 * 16], F32)
    bmods = Buf()
    cact = sb("cact", [128, 16], BF16)
    cTs = sb("cTs", [128, 16], F32)
    small = sb("small", [128, 16], F32)
    bsmall = Buf()
    lamrow = sb("lamrow", [1, 256], F32)
    lamtmp = sb("lamtmp", [1, 256], F32)
    lamcol = sb("lamcol", [128, 1], F32)
    negh = sb("negh", [128, 512], F32)
    bnegh = Buf()

    arena = sb("arena", [128, 8192], BF16)
    vst = arena[:, 0:4096].rearrange("p (a b) -> p a b", a=4)
    posi = arena[:, 4096:5120].bitcast(I32)
    cosT = arena[:, 5120:6144].bitcast(F32)
    sinT = arena[:, 6144:7168].bitcast(F32)
    kout = [arena[:, 7168:7680], arena[:, 7680:8192]]
    bposi = Buf()
    bcos, bsin = Buf(), Buf()
    qz = [sb("qz%d" % i, [128, 2, 512], BF16) for i in range(2)]
    bqz = [Buf() for _ in range(2)]
    bkout = [Buf() for _ in range(2)]
    bvst = [Buf() for _ in range(4)]
    uT = sb("uT", [128, 8, 544], BF16)
    buT = [Buf() for _ in range(8)]
    vh = [sb("vh%d" % i, [128, 32, 128], BF16) for i in range(2)]
    bvh = [Buf() for _ in range(2)]
    pt = [sb("pt%d" % i, [128, 512], BF16) for i in range(4)]
    bpt = [Buf() for _ in range(4)]
    masks = sb("masks", [128, 8, 512], BF16)
    diag = [arena[:, 0:3968].rearrange("p (a b) -> p a b", a=31), arena[:, 4096:4096 + 3968].rearrange("p (a b) -> p a b", a=31)]
    bdiag = [Buf() for _ in range(2)]
    convo = actT
    bconvo = bact
    hc = [sb("hc%d" % i, [128, 8, 32], BF16) for i in range(2)]
    bhc = [Buf() for _ in range(2)]
    hzero = sb("hzero", [128, 8, 32], BF16)
    kTh = [xs[i][:, :].bitcast(BF16).rearrange("p (r t) -> p r t", r=2) for i in range(2)]

    pbank = [nc.alloc_psum_tensor("pb%d" % i, [128, 512], F32) for i in range(8)]
    bpb = [Buf() for _ in range(8)]
    pbi = [0]

    def newps(lo=0, hi=8):
        n = hi - lo
        i = lo + pbi[0] % n
        pbi[0] += 1
        return pbank[i], bpb[i]

    sgen = P.newsem()
    sst = [P.newsem() for _ in range(4)]
    ssti = [0]

    def stsem():
        s = sst[ssti[0] % 4]
        ssti[0] += 1
        return s

    s_qzst = [P.newsem() for _ in range(2)]
    s_kost = [P.newsem() for _ in range(2)]
    s_vst = [P.newsem() for _ in range(4)]
    s_ut1, s_ut2, s_utl, s_hc0, s_hc1 = P.newsem(), P.newsem(), P.newsem(), P.newsem(), P.newsem()
    s_x1st, s_x1ld, s_posi = P.newsem(), P.newsem(), P.newsem()

    def mm(out, lhsT, rhs, start, stop, reads, wbuf, signal=None):
        P.op("pe", lambda e: e.matmul(out, lhsT=lhsT, rhs=rhs, start=start, stop=stop),
             reads=reads, writes=[wbuf], signal=(stop if signal is None else signal))

    def act(out, in_, func, reads, writes, bias=None, scale=None):
        kw = {}
        if bias is not None:
            kw["bias"] = bias
        if scale is not None:
            kw["scale"] = scale
        P.op("act", lambda e: e.activation(out=out, in_=in_, func=func, **kw), reads=reads, writes=writes)

    def tt(eng, out, in0, in1, op, reads, writes):
        P.op(eng, lambda e: e.tensor_tensor(out=out, in0=in0, in1=in1, op=op), reads=reads, writes=writes)

    def ts(eng, out, in0, s1, op0, reads, writes, s2=None, op1=None):
        if op1 is None and eng == "pool" and op0 == ALU.mult:
            P.op(eng, lambda e: e.tensor_scalar(out=out, in0=in0, scalar1=s1, scalar2=1.0, op0=ALU.mult, op1=ALU.mult),
                 reads=reads, writes=writes)
        elif op1 is None:
            P.op(eng, lambda e: e.tensor_scalar(out=out, in0=in0, scalar1=s1, scalar2=None, op0=op0),
                 reads=reads, writes=writes)
        else:
            P.op(eng, lambda e: e.tensor_scalar(out=out, in0=in0, scalar1=s1, scalar2=s2, op0=op0, op1=op1),
                 reads=reads, writes=writes)

    def stt(out, in0, scalar, in1, op0, op1, reads, writes):
        P.op("dve", lambda e: e.scalar_tensor_tensor(out=out, in0=in0, scalar=scalar, in1=in1, op0=op0, op1=op1),
             reads=reads, writes=writes)

    def cp(eng, out, in_, reads, writes):
        P.op(eng, lambda e: e.tensor_copy(out=out, in_=in_), reads=reads, writes=writes)

    epsc = sb("epsc", [128, 2], F32)
    bepsc = Buf()

    def eps_ap(eps):
        return epsc[:, 0:1] if eps == RMS_EPS else epsc[:, 1:2]

    rstd_p = sb("rstd_p", [128, 512], F32)
    brstd_p = Buf()
    mean_p = sb("mean_p", [128, 512], F32)
    bmean_p = Buf()

    def rsqrt_bcast(ps, bps, scale, eps, dst=None):
        t, bt = newtmp() if dst is None else dst
        act(t[:], ps[:], AF.Sqrt, [bps], [bt], scale=scale, bias=eps_ap(eps))
        P.op("dve", lambda e, t=t: e.reciprocal(out=t[:], in_=t[:]), reads=[bt], writes=[bt])
        return t, bt

    units = []

    def plan_units():
        wav = wada_d.ap().rearrange("(k p) n -> p k n", p=128)
        def ada_units(lo, hi):
            for g in range(lo, hi):
                units.append(("ada", wav[:, :, g * 256:(g + 1) * 256], 16, 256))

        ada_units(0, 40)

        def ffn(l):
            wg = wgu_d[l].ap().rearrange("(k p) n -> p k n", p=128)
            wd = wdn_d[l].ap().rearrange("(j p) n -> p j n", p=128)
            for grp in range(6):
                nu = 4 if grp < 5 else 2
                for u in range(nu):
                    c0 = (grp * 4 + u) * 256
                    units.append(("gu", wg[:, :, c0:c0 + 256], 16, 256))
                    units.append(("gu", wg[:, :, FF + c0:FF + c0 + 256], 16, 256))
                nf = nu * 2
                for cg in range(4):
                    units.append(("dn", wd[:, grp * 8:grp * 8 + nf, cg * 512:(cg + 1) * 512], nf, 512))

        wi = win_d.ap().rearrange("(k p) n -> p k n", p=128)
        wo = wout_d.ap().rearrange("(k p) n -> p k n", p=128)
        for j in range(NBLK):
            ffn(0)
            order = list(range(12))
            for a in range(4):
                order += [12 + a, 16 + a]
            for u in order:
                units.append(("in", wi[:, :, u * 256:(u + 1) * 256], 16, 256))
        ada_units(40, 72)
        for j in range(NBLK):
            for u in range(8):
                units.append(("out", wo[:, :, u * 256:(u + 1) * 256], 16, 256))
            ffn(1)

    plan_units()
    wstate = {"issued": 0, "next": 0}

    def wissue(upto):
        upto = min(upto, len(units))
        while wstate["issued"] < upto:
            i = wstate["issued"]
            tag, src, k, n = units[i]
            s = i % NSLOT
            dst = wr[s][:, 0:k * n].rearrange("p (k n) -> p k n", k=k)
            P.dma("pool", dst, src, reads=[bW], writes=[bwr[s]], sem=swr[s])
            wstate["issued"] += 1

    def wnext(tag):
        i = wstate["next"]
        assert units[i][0] == tag, (units[i][0], tag, i)
        wissue(i + NSLOT - 1)
        wstate["next"] += 1
        _, _, k, n = units[i]
        s = i % NSLOT
        return wr[s][:, 0:k * n].rearrange("p (k n) -> p k n", k=k), bwr[s]

    if shard_w is True:
        sgw = P.newsem()
        sgb = P.newsem()
        bWb = Buf()
        for sh, full in wgather:
            shi = nc.dram_tensor(sh.name + "_shi", list(sh.shape), F32)
            P.dma("sp", shi.ap(), sh.ap(), writes=[bWb], sem=sgb)
            P.custom("pool", lambda e, shi=shi, full=full: e.collective_compute(
                "AllGather", ALU.bypass, replica_groups=[list(range(8))], ins=[shi.ap()], outs=[full.ap()]),
                reads=[bWb], writes=[bW], sem=sgw)
    P.dma("sp", identf[:], identf_d.ap(), writes=[bconst], sem=sgen)
    P.dma("sp", cb16[:], cb16_d.ap().rearrange("p (a b) -> p a b", a=4), writes=[bconst], sem=sgen)
    P.dma("sp", vecs[:], vecs_d.ap(), writes=[bconst], sem=sgen)
    P.dma("sp", badaT_s[:], badaT_d.ap(), writes=[bconst], sem=sgen)
    P.dma("sp", cTs[:], cT_d.ap(), writes=[bconst], sem=sgen)
    P.dma("sp", lamrow[:], lam_d.ap(), writes=[bconst], sem=sgen)
    P.dma("sp", masks[:], masks_d.ap().rearrange("p (a b) -> p a b", a=8), writes=[bconst], sem=sgen)
    bconst.w = (sgen, P.dcnt[sgen], "dma", None)
    if debug_stop is not None:
        P.dma("sp", posi, pos_d.ap()[0:1, 0:512].broadcast_to([128, 512]), writes=[bposi], sem=s_posi)
    wissue(NSLOT)
    P.op("dve", lambda e: e.memset(negh[:], -0.5), writes=[bnegh])
    P.op("dve", lambda e: e.memset(epsc[:, 0:1], RMS_EPS), writes=[bepsc])
    P.op("dve", lambda e: e.memset(epsc[:, 1:2], LN_EPS), writes=[bepsc])
    P.op("dve", lambda e: e.memset(onesf[:], 1.0), writes=[bsmall])
    P.op("dve", lambda e: e.memset(hzero[:], 0.0), writes=[bsmall])
    P.op("dve", lambda e: e.memset(lamcol[:], 0.0), writes=[bsmall])
    for i in range(2):
        P.op("pool", lambda e, i=i: e.memset(qz[i][:], 0.0), writes=[bqz[i]])
    act(cact[:], cTs[:], AF.Silu, [bconst], [bsmall])
    if "nolam" in flags:
        P.barrier()
        P.emit()
        return nc
    tt("dve", lamtmp[:, 0:64], lamrow[:, 0:64], lamrow[:, 64:128], ALU.mult, [bconst], [bsmall])
    tt("dve", lamtmp[:, 64:128], lamrow[:, 128:192], lamrow[:, 192:256], ALU.mult, [bconst], [bsmall])
    P.op("dve", lambda e: e.reduce_sum(out=lamtmp[:, 128:130], in_=lamtmp[:, 0:128].rearrange("p (a b) -> p a b", a=2),
                                       axis=mybir.AxisListType.X), reads=[bsmall], writes=[bsmall])
    act(lamtmp[:, 130:132], lamtmp[:, 128:130], AF.Exp, [bsmall], [bsmall])
    ts("dve", lamtmp[:, 132:133], lamtmp[:, 131:132], lamtmp[:, 130:131], ALU.subtract, [bsmall], [bsmall],
       s2=-LAM_INIT, op1=ALU.add)
    cp("dve", lamcol[0:1, 0:1], lamtmp[:, 132:133], [bsmall], [bsmall])
    psb, bps_ = newps()
    P.op("pe", lambda e: e.matmul(psb[:, 0:1], lhsT=onesf[:], rhs=lamcol[:], start=True, stop=True),
         reads=[bsmall], writes=[bps_])
    cp("dve", small[:, 0:1], psb[:, 0:1], [bps_], [bsmall])
    ts("dve", small[:, 1:2], vecs[:, V_SUB:V_SUB + 1], 1.0 - LAM_INIT, ALU.mult, [bconst], [bsmall])

    def ada_part(g_lo, g_hi):
        psa, bpsa = newps()
        for g in range(g_lo, g_hi):
            w, bw = wnext("ada")
            for cc in range(2):
                col = g * 2 + cc
                for k in range(16):
                    P.op("pe", lambda e, w=w, k=k, cc=cc, col=col: e.matmul(
                        psa[:, col:col + 1], lhsT=w[:, k, cc * 128:(cc + 1) * 128], rhs=cact[:, k:k + 1],
                        start=(k == 0), stop=(k == 15)), reads=[bw, bsmall], writes=[bpsa], signal=(k == 15))
        c0, c1 = g_lo * 2, g_hi * 2
        tt("dve", adaT[:, c0:c1], psa[:, c0:c1], badaT_s[:, c0:c1], ALU.add, [bpsa, bconst], [bmods])

    def ada_ap(n):
        return adaT[:, n * 16:(n + 1) * 16]

    def mods_ap(n):
        return mods[:, n * 16:(n + 1) * 16]

    def mods_AB(si, vn):
        stt(mods_ap(3 * si), ada_ap(3 * si + 1), 1.0, vecs[:, vn:vn + 16], ALU.add, ALU.mult, [bmods, bconst], [bmods])
        cp("dve", mods_ap(3 * si + 1), ada_ap(3 * si), [bmods], [bmods])

    def mods_G(si, half):
        ts("dve", mods_ap(3 * si + 2), ada_ap(3 * si + 2), half, ALU.mult, [bmods], [bmods])

    if "noada" not in flags:
        ada_part(0, 40)
        mods_AB(0, V_N1)
        mods_G(0, 0.5)
        mods_AB(1, V_NM)

    def ada_tail():
        ada_part(40, 72)
        mods_G(1, 1.0)
        mods_AB(2, V_N2)
        mods_G(2, 0.5)

    def modv(si, which, k):
        c = (3 * si + which) * 16 + k
        return mods[:, c:c + 1]

    xv = x_d.ap()
    ov = out_d.ap()
    xload_state = {"n": 0}

    def load_xtile(gt):
        s = gt % 2
        P.dma("sp", xs[s][:], xv[gt * 128:(gt + 1) * 128, :], writes=[bxs[s]], sem=sxs[s])

    def transpose_in(j):
        for t in range(4):
            gt = j * 4 + t
            s = gt % 2
            for g in range(4):
                ps, bps = newps()
                for dk in range(4):
                    k = 4 * g + dk
                    P.op("pe", lambda e, ps=ps, s=s, k=k, dk=dk: e.transpose(
                        ps[:, dk * 128:(dk + 1) * 128], xs[s][:, k * 128:(k + 1) * 128], identf[:]),
                        reads=[bxs[s], bconst], writes=[bps], signal=(dk == 3))
                act(xT[:, 4 * g:4 * g + 4, t * 128:(t + 1) * 128], ps[:].rearrange("p (a b) -> p a b", a=4),
                    AF.Copy, [bps], [bxT[4 * g + i] for i in range(4)])
            nxt = gt + 2
            if nxt < 16 and nxt < j * 4 + 6:
                load_xtile(nxt)

    def norm_mod(si):
        ps, bps = newps()
        for k in range(16):
            sq, bsq = newsq()
            tt("dve", sq[:], xT[:, k, :], xT[:, k, :], ALU.mult, [bxT[k]], [bsq])
            mm(ps[:], onesb, sq[:], k == 0, k == 15, [bsq, bconst], bps, signal=True)
        rstd, brs = rsqrt_bcast(ps, bps, 1.0 / D, RMS_EPS, dst=(rstd_p, brstd_p))
        for k in range(16):
            t, bt = newtmp()
            stt(t[:], xT[:, k, :], modv(si, 0, k), rstd[:], ALU.mult, ALU.mult, [bxT[k], bmods, brs], [bt])
            act(hT[:, k, :], t[:], AF.Identity, [bt, bmods], [bhT[k]], bias=modv(si, 1, k))

    def ffn(si):
        for grp in range(6):
            nu = 4 if grp < 5 else 2
            for u in range(nu):
                wg, bwg = wnext("gu")
                wu, bwu = wnext("gu")
                for cc in range(2):
                    fi = u * 2 + cc
                    psg, bpsg = newps()
                    psu, bpsu = newps()
                    for k in range(16):
                        mm(psg[:], wg[:, k, cc * 128:(cc + 1) * 128], hT[:, k, :], k == 0, k == 15, [bwg, bhT[k]], bpsg)
                    for k in range(16):
                        mm(psu[:], wu[:, k, cc * 128:(cc + 1) * 128], hT[:, k, :], k == 0, k == 15, [bwu, bhT[k]], bpsu)
                    t, bt = newtmp()
                    act(t[:], psg[:], AF.Silu, [bpsg], [bt])
                    tt("dve", actT[:, fi, :], t[:], psu[:], ALU.mult, [bt, bpsu], [bact[fi]])
            nf = nu * 2
            for cg in range(4):
                wd, bwd = wnext("dn")
                for dc in range(4):
                    k = cg * 4 + dc
                    ps, bps = newps()
                    for fi in range(nf):
                        mm(ps[:], wd[:, fi, dc * 128:(dc + 1) * 128], actT[:, fi, :], fi == 0, fi == nf - 1,
                           [bwd, bact[fi]], bps)
                    stt(xT[:, k, :], ps[:], modv(si, 2, k), xT[:, k, :], ALU.mult, ALU.add, [bps, bmods, bxT[k]], [bxT[k]])

    def rope_tables(j):
        P.dma("sp", posi, pos_d.ap()[0:1, j * 512:(j + 1) * 512].broadcast_to([128, 512]), writes=[bposi], sem=s_posi)
        ang, bang = newtmp()
        cp("dve", ang[:], posi, [bposi], [bang])
        ts("dve", ang[:], ang[:], vecs[:, V_INVF:V_INVF + 1], ALU.mult, [bang, bconst], [bang])
        kk, bkk = newtmp()
        ki, bki = newtmp()
        ts("dve", kk[:], ang[:], 1.0 / TWO_PI, ALU.mult, [bang], [bkk])
        kiv = ki[:].bitcast(I32)
        cp("dve", kiv, kk[:], [bkk], [bki])
        cp("dve", kk[:], kiv, [bki], [bkk])
        r, br = newtmp()
        stt(r[:], kk[:], -C1, ang[:], ALU.mult, ALU.add, [bkk, bang], [br])
        stt(r[:], kk[:], -C2, r[:], ALU.mult, ALU.add, [bkk, br], [br])
        m_, bm_ = newtmp()
        ts("dve", m_[:], r[:], math.pi, ALU.is_gt, [br], [bm_], s2=-TWO_PI, op1=ALU.mult)
        tt("dve", r[:], r[:], m_[:], ALU.add, [br, bm_], [br])
        ts("dve", m_[:], r[:], -math.pi, ALU.is_lt, [br], [bm_], s2=TWO_PI, op1=ALU.mult)
        tt("dve", r[:], r[:], m_[:], ALU.add, [br, bm_], [br])
        act(sinT, r[:], AF.Sin, [br], [bsin])
        ts("dve", sinT, sinT, vecs[:, V_SSIGN:V_SSIGN + 1], ALU.mult, [bsin, bconst], [bsin])
        c_, bc_ = newtmp()
        ts("dve", c_[:], r[:], math.pi / 2, ALU.add, [br], [bc_])
        ts("dve", m_[:], c_[:], math.pi, ALU.is_gt, [bc_], [bm_], s2=-TWO_PI, op1=ALU.mult)
        tt("dve", c_[:], c_[:], m_[:], ALU.add, [bc_, bm_], [bc_])
        act(cosT, c_[:], AF.Sin, [bc_], [bcos])

    def qk_chunk(ps, bps, gcol):
        qb, bqb = newqb()
        act(qb[:], ps[:], AF.Identity, [bps, bconst], [bqb], scale=vecs[:, gcol:gcol + 1])
        sq, bsq = newsq()
        act(sq[:], ps[:], AF.Square, [bps], [bsq])
        psA, bpsA = newps()
        mm(psA[:], blkones, sq[:], True, True, [bsq, bconst], bpsA)
        psB, bpsB = newps()
        mm(psB[:], pswap, qb[:], True, True, [bqb, bconst], bpsB)
        rstd, brs = rsqrt_bcast(psA, bpsA, 1.0 / 64, RMS_EPS)
        t1, bt1 = newtmp()
        tt("dve", t1[:], qb[:], cosT, ALU.mult, [bqb, bcos], [bt1])
        t2, bt2 = newtmp()
        tt("dve", t2[:], psB[:], sinT, ALU.mult, [bpsB, bsin], [bt2])
        tt("pool", t1[:], t1[:], t2[:], ALU.add, [bt1, bt2], [bt1])
        return t1, bt1, rstd, brs

    def w_in_block(j):
        for u in range(4):
            w, bw = wnext("in")
            for cc in range(2):
                hd = u * 2 + cc
                ps, bps = newps()
                for k in range(16):
                    mm(ps[:], w[:, k, cc * 128:(cc + 1) * 128], hT[:, k, :], k == 0, k == 15, [bw, bhT[k]], bps)
                t1, bt1, rstd, brs = qk_chunk(ps, bps, V_GQ)
                z = hd % 2
                tt("dve", qz[z][0:64, 0, :], t1[0:64, :], rstd[0:64, :], ALU.mult, [bt1, brs], [bqz[z]])
                tt("pool", qz[z][64:128, 1, :], t1[64:128, :], rstd[64:128, :], ALU.mult, [bt1, brs], [bqz[z]])
                P.dma("sp", QT_d.ap()[j * 128:(j + 1) * 128, hd * 1024:(hd + 1) * 1024],
                      qz[z][:].rearrange("p a b -> p (a b)"), reads=[bqz[z]], writes=[bQT[j]], sem=s_qzst[z])
        for u in range(4):
            w, bw = wnext("in")
            for cc in range(2):
                hd = u * 2 + cc
                ps, bps = newps()
                for k in range(16):
                    mm(ps[:], w[:, k, cc * 128:(cc + 1) * 128], hT[:, k, :], k == 0, k == 15, [bw, bhT[k]], bps)
                t1, bt1, rstd, brs = qk_chunk(ps, bps, V_GK)
                z = hd % 2
                tt("dve", kout[z], t1[:], rstd[:], ALU.mult, [bt1, brs], [bkout[z]])
                P.dma("sp", KTo_d[hd // 4].ap()[(hd % 4) * 128:(hd % 4 + 1) * 128, j * 512:(j + 1) * 512], kout[z],
                      reads=[bkout[z]], writes=[bKTo], sem=s_kost[z])
        for u in range(4):
            w, bw = wnext("in")
            for t in range(4):
                ps, bps = newps()
                for k in range(16):
                    mm(ps[:, 0:256], hT[:, k, t * 128:(t + 1) * 128], w[:, k, :], k == 0, k == 15, [bw, bhT[k]], bps)
                act(vst[:, t, u * 256:(u + 1) * 256], ps[:, 0:256], AF.Copy, [bps], [bvst[t]])
        for t in range(4):
            P.dma("sp", Vo_d[j // 2].ap()[(j % 2) * 512 + t * 128:(j % 2) * 512 + (t + 1) * 128, :], vst[:, t, :],
                  reads=[bvst[t]], writes=[bVo], sem=s_vst[t])
        for a in range(4):
            wa, bwa = wnext("in")
            wg, bwg = wnext("in")
            for cc in range(2):
                c = a * 2 + cc
                psa_, bpsa_ = newps()
                psg_, bpsg_ = newps()
                for k in range(16):
                    mm(psa_[:], wa[:, k, cc * 128:(cc + 1) * 128], hT[:, k, :], k == 0, k == 15, [bwa, bhT[k]], bpsa_)
                for k in range(16):
                    mm(psg_[:], wg[:, k, cc * 128:(cc + 1) * 128], hT[:, k, :], k == 0, k == 15, [bwg, bhT[k]], bpsg_)
                t, bt = newtmp()
                act(t[:], psg_[:], AF.Sigmoid, [bpsg_], [bt])
                tt("dve", uT[:, c, 32:544], t[:], psa_[:], ALU.mult, [bt, bpsa_], [buT[c]])
        P.dma("sp", UT_d.ap()[j * 128:(j + 1) * 128, :].rearrange("p (a b) -> p a b", a=8), uT[:, :, 32:544],
              reads=buT, writes=[bUT[j]], sem=s_ut1)
        P.dma("sp", Ho_d.ap()[j * 1024:(j + 1) * 1024, :].rearrange("(c p) t -> p c t", p=128), uT[:, :, 512:544],
              reads=buT, writes=[bHo], sem=s_ut2)

    load_xtile(0)
    load_xtile(1)
    for j in range(NBLK):
        transpose_in(j)
        if debug_stop == "xT":
            break
        norm_mod(0)
        ffn(0)
        P.dma("sp", X1_d.ap()[j * 128:(j + 1) * 128, :].rearrange("p (a b) -> p a b", a=16), xT[:],
              reads=bxT, writes=[bX1[j]], sem=s_x1st)
        norm_mod(1)
        rope_tables(j)
        w_in_block(j)

    RG = [[0, 1], [2, 3], [4, 5], [6, 7]]
    sc1, sc2, sc3 = P.newsem(), P.newsem(), P.newsem()
    if "noxchg" not in flags:
        P.custom("pool", lambda e: e.collective_compute("AllGather", ALU.bypass, replica_groups=RG,
                                                        ins=[Ho_d.ap()], outs=[Ha_d.ap()]),
                 reads=[bHo], writes=[bHa], sem=sc1)
        for i in range(2):
            P.custom("pool", lambda e, i=i: e.collective_compute("AllGather", ALU.bypass, replica_groups=RG,
                                                            ins=[KTo_d[i].ap()], outs=[KTa_d[i].ap()]),
                     reads=[bKTo], writes=[bKTa], sem=sc2)
            P.custom("pool", lambda e, i=i: e.collective_compute("AllGather", ALU.bypass, replica_groups=RG,
                                                            ins=[Vo_d[i].ap()], outs=[Va_d[i].ap()]),
                     reads=[bVo], writes=[bVa], sem=sc3)
    if debug_stop is None:
        ada_tail()
    P.barrier()

    skt = [P.newsem() for _ in range(2)]
    svh2 = [[P.newsem() for _ in range(4)] for _ in range(2)]
    bvh4 = [[Buf() for _ in range(4)] for _ in range(2)]
    sqz = [P.newsem() for _ in range(2)]
    KTv = [KTa_d[i].ap().rearrange("(r q) t -> q r t", r=2) for i in range(2)]
    Vv = [Va_d[i].ap().rearrange("(r n p) d -> p r n d", r=2, p=128) for i in range(2)]
    Hv = Ha_d.ap().rearrange("(r j c p) t -> r j p c t", r=2, j=NBLK, p=128)

    def load_head(j, hd):
        z = hd % 2
        P.dma("sp", qz[z][:].rearrange("p a b -> p (a b)"), QT_d.ap()[j * 128:(j + 1) * 128, hd * 1024:(hd + 1) * 1024],
              reads=[bQT[j]], writes=[bqz[z]], sem=sqz[z])
        P.dma("sp", kTh[z], KTv[hd // 4][(hd % 4) * 128:(hd % 4 + 1) * 128, :, :], reads=[bKTa], writes=[bxs[z]], sem=skt[z])
        vhv = vh[z][:].rearrange("p (r n) d -> p r n d", r=2)
        for i in range(2):
            for r in range(2):
                P.dma("sp", vhv[:, r, i * 8:(i + 1) * 8, :], Vv[i][:, r, :, hd * 128:(hd + 1) * 128],
                      reads=[bVa], writes=[bvh4[z][i * 2 + r]], sem=svh2[z][i * 2 + r])

    def attention_block(j):
        load_head(j, 0)
        nkb = 8 * j + 8
        for hd in range(8):
            z = hd % 2
            if hd + 1 < 8:
                load_head(j, hd + 1)
            ulist = [(kb, m) for kb in range(nkb) for m in range(2)]
            Sps = {}

            def issue_S(ui):
                kb, m = ulist[ui]
                G = kb // 4
                r = G % 2
                lt = (G // 2) * 4 + kb % 4
                ps, bps = newps(0, 4)
                mm(ps[:], kTh[z][:, r, lt * 128:(lt + 1) * 128], qz[z][:, m, :], True, True, [bxs[z], bqz[z]], bps)
                Sps[ui] = (ps, bps)

            issue_S(0)
            issue_S(1)
            for ui, (kb, m) in enumerate(ulist):
                if ui + 2 < len(ulist):
                    issue_S(ui + 2)
                ps, bps = Sps.pop(ui)
                G = kb // 4
                r = G % 2
                lt = (G // 2) * 4 + kb % 4
                pi = ui % 4
                act(pt[pi][:], ps[:], AF.Exp, [bps], [bpt[pi]], scale=0.125)
                if kb >= 8 * j:
                    eng = "dve" if (ui % 2 == 0) else "pool"
                    tt(eng, pt[pi][:], pt[pi][:], masks[:, kb - 8 * j, :], ALU.mult, [bpt[pi], bconst], [bpt[pi]])
                mm(pbank[4 + 2 * m][:], vh[z][:, r * 16 + lt, :], pt[pi][:], kb == 0, kb == nkb - 1,
                   [bvh4[z][(lt // 8) * 2 + r], bpt[pi]], bpb[4 + 2 * m])
                mm(pbank[5 + 2 * m][:], onesb, pt[pi][:], kb == 0, kb == nkb - 1, [bconst, bpt[pi]], bpb[5 + 2 * m])
            r0, br0 = newtmp()
            P.op("dve", lambda e, r0=r0: e.reciprocal(out=r0[:], in_=pbank[5][:]), reads=[bpb[5]], writes=[br0])
            o, bo = newtmp()
            tt("dve", o[:], pbank[4][:], r0[:], ALU.mult, [bpb[4], br0], [bo])
            r1, br1 = newtmp()
            P.op("dve", lambda e, r1=r1: e.reciprocal(out=r1[:], in_=pbank[7][:]), reads=[bpb[7]], writes=[br1])
            o1, bo1 = newtmp()
            tt("dve", o1[:], pbank[6][:], r1[:], ALU.mult, [bpb[6], br1], [bo1])
            stt(o[:], o1[:], small[:, 0:1], o[:], ALU.mult, ALU.add, [bo1, bsmall, bo], [bo])
            sq, bsq = newsq()
            tt("pool", sq[:], o[:], o[:], ALU.mult, [bo], [bsq])
            psS, bpsS = newps(0, 4)
            mm(psS[:], onesb, sq[:], True, True, [bsq, bconst], bpsS)
            rstd, brs = rsqrt_bcast(psS, bpsS, 1.0 / 128, RMS_EPS)
            stt(hT[:, hd, :], o[:], small[:, 1:2], rstd[:], ALU.mult, ALU.mult, [bo, bsmall, brs], [bhT[hd]])

    def conv_block(j):
        P.dma("sp", uT[:, :, 32:544], UT_d.ap()[j * 128:(j + 1) * 128, :].rearrange("p (a b) -> p a b", a=8),
              reads=[bUT[j]], writes=buT, sem=s_utl)
        P.dma("sp", hc[0][:], Hv[0, j], reads=[bHa], writes=[bhc[0]], sem=s_hc0)
        if j > 0:
            P.dma("sp", hc[1][:], Hv[1, j - 1], reads=[bHa], writes=[bhc[1]], sem=s_hc1)
            h1 = hc[1]
        else:
            h1 = hzero
        ht, bht = newtmp()
        htv = ht[:, 0:256].rearrange("p (a b) -> p a b", a=8)
        ts("dve", htv, hc[0][:], vecs[:, V_SEL0:V_SEL0 + 1], ALU.mult, [bhc[0], bconst], [bht])
        stt(uT[:, :, 0:32], h1[:], vecs[:, V_SEL1:V_SEL1 + 1], htv, ALU.mult, ALU.add, [bhc[1], bconst, bht, bsmall], buT)
        ps1, bps1 = newps(4, 6)
        ps2, bps2 = pbank[6], bpb[6]
        if ps1 is pbank[4]:
            ps2, bps2 = pbank[7], bpb[7]
        for c in range(8):
            dz = c % 2
            for k in range(31):
                ts("dve" if k % 2 == 0 else "pool", diag[dz][:, k, :], identb, vecs[:, V_CW + c * 31 + k:V_CW + c * 31 + k + 1],
                   ALU.mult, [bconst], [bdiag[dz]])
            ps, bps = newps(0, 4)
            for k in range(31):
                mm(ps[:], diag[dz][:, k, :], uT[:, c, 2 + k:2 + k + 512], k == 0, k == 30, [bdiag[dz], buT[c]], bps)
            act(convo[:, c, :], ps[:], AF.Identity, [bps, bconst], [bconvo[c]], bias=vecs[:, V_CB + c:V_CB + c + 1])
            sq, bsq = newsq()
            tt("dve", sq[:], convo[:, c, :], convo[:, c, :], ALU.mult, [bconvo[c]], [bsq])
            mm(ps1[:], onesb, convo[:, c, :], c == 0, c == 7, [bconvo[c], bconst], bps1, signal=True)
            mm(ps2[:], onesb, sq[:], c == 0, c == 7, [bsq, bconst], bps2, signal=True)
        mean, bmean = mean_p, bmean_p
        ts("dve", mean[:], ps1[:], 1.0 / 1024, ALU.mult, [bps1], [bmean])
        var, bvar = rstd_p, brstd_p
        ts("dve", var[:], ps2[:], 1.0 / 1024, ALU.mult, [bps2], [bvar])
        m2, bm2 = newtmp()
        tt("dve", m2[:], mean[:], mean[:], ALU.mult, [bmean], [bm2])
        tt("dve", var[:], var[:], m2[:], ALU.subtract, [bvar, bm2], [bvar])
        act(var[:], var[:], AF.Sqrt, [bvar], [bvar], bias=eps_ap(LN_EPS))
        P.op("dve", lambda e: e.reciprocal(out=var[:], in_=var[:]), reads=[bvar], writes=[bvar])
        for c in range(8):
            t, bt = newtmp()
            tt("dve" if c % 2 == 0 else "pool", t[:], convo[:, c, :], mean[:], ALU.subtract, [bconvo[c], bmean], [bt])
            tt("pool" if c % 2 == 0 else "dve", t[:], t[:], var[:], ALU.mult, [bt, bvar], [bt])
            act(hT[:, 8 + c, :], t[:], AF.Silu, [bt, bconst], [bhT[8 + c]],
                scale=vecs[:, V_LG + c:V_LG + c + 1], bias=vecs[:, V_LB + c:V_LB + c + 1])

    def w_out_block():
        for u in range(8):
            w, bw = wnext("out")
            for cc in range(2):
                k = u * 2 + cc
                ps, bps = newps()
                for ke in range(16):
                    mm(ps[:], w[:, ke, cc * 128:(cc + 1) * 128], hT[:, ke, :], ke == 0, ke == 15, [bw, bhT[ke]], bps)
                stt(xT[:, k, :], ps[:], modv(1, 2, k), xT[:, k, :], ALU.mult, ALU.add, [bps, bmods, bxT[k]], [bxT[k]])

    def transpose_out(j):
        for t in range(4):
            gt = j * 4 + t
            s = gt % 2
            for g in range(4):
                ps, bps = newps()
                for dk in range(4):
                    k = 4 * g + dk
                    P.op("pe", lambda e, ps=ps, k=k, dk=dk, t=t: e.transpose(
                        ps[:, dk * 128:(dk + 1) * 128], xT[:, k, t * 128:(t + 1) * 128], identf[:]),
                        reads=[bxT[k], bconst], writes=[bps], signal=(dk == 3))
                act(xs[s][:, g * 512:(g + 1) * 512], ps[:], AF.Copy, [bps], [bxs[s]])
            P.dma("sp", ov[gt * 128:(gt + 1) * 128, :], xs[s][:], reads=[bxs[s]], sem=sxs[s])

    if debug_stop is None:
        for j in range(NBLK):
            attention_block(j)
            conv_block(j)
            P.dma("sp", xT[:], X1_d.ap()[j * 128:(j + 1) * 128, :].rearrange("p (a b) -> p a b", a=16),
                  reads=[bX1[j]], writes=bxT, sem=s_x1ld)
            w_out_block()
            norm_mod(2)
            ffn(2)
            transpose_out(j)
    else:
        transpose_out(0)

    if debug_dump:
        sdd = P.newsem()
        P.barrier()
        for nm, t in (("X1", X1_d), ("QT", QT_d), ("UT", UT_d), ("KTa0", KTa_d[0]), ("KTa1", KTa_d[1]), ("Va0", Va_d[0]), ("Va1", Va_d[1]), ("Ha", Ha_d)):
            o = nc.dram_tensor("dbg_" + nm, list(t.shape), t.dtype, kind="ExternalOutput")
            P.dma("sp", o.ap(), t.ap(), sem=sdd)
    P.barrier()
    stuck = P.check_deadlock()
    assert not stuck, ("deadlock", stuck)
    P.emit()
    nc._prog_stats = {e: len(P.ops[e]) for e in P.ENG}
    return nc


def _host_prep(inputs, shard_w=False):
    f32 = np.float32
    x = np.asarray(inputs["x"], f32)
    c = np.asarray(inputs["c"], f32)
    pos = np.asarray(inputs["positions"], np.int32)

    def pk(v):
        return np.ascontiguousarray(np.asarray(v, f32).reshape(-1, 128).T)

    vecs = np.zeros((128, NV), f32)
    vecs[:, V_N1:V_N1 + 16] = pk(inputs["ffn1_norm"][0])
    vecs[:, V_NM:V_NM + 16] = pk(inputs["mix_norm"][0])
    vecs[:, V_N2:V_N2 + 16] = pk(inputs["ffn2_norm"][0])
    vecs[:, V_CB:V_CB + 8] = pk(inputs["conv_b"][0])
    vecs[:, V_LG:V_LG + 8] = pk(inputs["conv_ln_g"][0])
    vecs[:, V_LB:V_LB + 8] = pk(inputs["conv_ln_b"][0])
    vecs[:, V_GQ] = np.tile(np.asarray(inputs["q_norm"][0], f32), 2)
    vecs[:, V_GK] = np.tile(np.asarray(inputs["k_norm"][0], f32), 2)
    vecs[:, V_SUB] = np.asarray(inputs["subln"][0], f32)
    invf = (np.float32(10000.0) ** (-np.arange(0, 64, 2, dtype=f32) / np.float32(64))).astype(f32)
    vecs[:, V_INVF] = np.tile(invf, 4)
    vecs[:, V_SSIGN] = np.tile(np.concatenate([-np.ones(32, f32), np.ones(32, f32)]), 2)
    cw = np.asarray(inputs["conv_w"][0], f32)
    vecs[:, V_CW:V_CW + 248] = cw.reshape(31, 8, 128).transpose(2, 1, 0).reshape(128, 248)
    lam4 = np.concatenate([np.asarray(inputs[k][0], f32) for k in ("lambda_q1", "lambda_k1", "lambda_q2", "lambda_k2")])[None, :]
    b_adaT = np.ascontiguousarray(np.asarray(inputs["b_ada"][0], f32).reshape(144, 128).T)
    ident = np.eye(128, dtype=f32)
    ones = np.ones((128, 128), f32)
    blk = np.kron(np.eye(2, dtype=f32), np.ones((64, 64), f32))
    idx = np.arange(128)
    psw = np.zeros((128, 128), f32)
    psw[idx ^ 32, idx] = 1.0
    cb16 = np.stack([ident, ones, blk, psw], axis=1).reshape(128, 512).astype(ml_dtypes.bfloat16)
    kk = (np.arange(8)[None, :, None] * 128 + np.arange(128)[:, None, None])
    in_maps = []
    shared = {
        "w_ada": np.asarray(inputs["w_ada"][0], f32), "b_adaT": b_adaT,
        "w_gu1": np.asarray(inputs["ffn1_w_gu"][0], f32), "w_gu2": np.asarray(inputs["ffn2_w_gu"][0], f32),
        "w_down1": np.asarray(inputs["ffn1_w_down"][0], f32), "w_down2": np.asarray(inputs["ffn2_w_down"][0], f32),
        "w_in": np.asarray(inputs["w_in"][0], f32), "w_out": np.asarray(inputs["w_out"][0], f32),
        "lam4": lam4, "cb16": cb16, "identf": ident,
    }
    for core in range(8):
        b, h = core // 2, core % 2
        xl = np.ascontiguousarray(x[b].reshape(8, 512, D)[h::2].reshape(NTOK, D))
        pl = np.ascontiguousarray(pos[b].reshape(8, 512)[h::2].reshape(1, NTOK))
        v = vecs.copy()
        v[:, V_SEL0] = 1.0 if h == 1 else 0.0
        v[:, V_SEL1] = 1.0 if h == 0 else 0.0
        qq = h * 512 + np.arange(512)[None, None, :]
        mask = (kk <= qq).astype(f32).reshape(128, 8 * 512).astype(ml_dtypes.bfloat16)
        m = dict(shared)
        if shard_w == "none":
            for wn in ("w_ada", "w_gu1", "w_gu2", "w_down1", "w_down2", "w_in", "w_out"):
                del m[wn]
        elif shard_w:
            for wn in ("w_ada", "w_gu1", "w_gu2", "w_down1", "w_down2", "w_in", "w_out"):
                rr = shared[wn].shape[0] // 8
                m[wn] = np.ascontiguousarray(shared[wn][core * rr:(core + 1) * rr])
        m.update({"x": xl, "pos": pl, "cT": pk(c[b]), "vecs": v, "masks": mask})
        in_maps.append(m)
    return in_maps


_NC_CACHE = {}


def kernel(**inputs):
    import os
    dump = os.environ.get("MK_DUMP") == "1"
    in_maps = _host_prep(inputs)
    key = "nc_dump" if dump else "nc"
    if key not in _NC_CACHE:
        _NC_CACHE[key] = build_program(debug_dump=dump)
    nc = _NC_CACHE[key]
    res = run_bass_kernel_spmd(nc, in_maps, core_ids=list(range(8)))
    out = np.zeros((4, 4096, D), np.float32)
    for core in range(8):
        b, h = core // 2, core % 2
        y = np.asarray(res.results[core]["out"], np.float32).reshape(NBLK, 512, D)
        out[b].reshape(8, 512, D)[h::2] = y
    if dump:
        dd = os.environ.get("MK_DUMP_DIR", "/tmp")
        for core in range(2):
            for nm in ("X1", "QT", "UT", "KTa0", "KTa1", "Va0", "Va1", "Ha"):
                np.save(os.path.join(dd, "dbg_%s_%d.npy" % (nm, core)),
                        np.asarray(res.results[core]["dbg_" + nm]).astype(np.float32))
        np.save(os.path.join(dd, "out.npy"), out)
    return out
```

```python
import math
import numpy as np
import ml_dtypes
import concourse.bass as bass
import concourse.mybir as mybir
from concourse.bass_utils import run_bass_kernel_spmd

F32 = mybir.dt.float32
BF16 = mybir.dt.bfloat16
I32 = mybir.dt.int32
ALU = mybir.AluOpType
AF = mybir.ActivationFunctionType

D = 2048
FF = 5632
NBLK = 4
TB = 512
NTOK = NBLK * TB
NADA = 9 * D
RMS_EPS = 1e-6
LN_EPS = 1e-5
LAM_INIT = 0.8 - 0.6 * math.exp(-0.3 * 0)
NSLOT = 6
TWO_PI = 2.0 * math.pi
C1 = 6.28125
C2 = TWO_PI - C1

V_N1, V_NM, V_N2 = 0, 16, 32
V_CB, V_LG, V_LB = 48, 56, 64
V_GQ, V_GK, V_SUB, V_INVF, V_SEL0, V_SEL1, V_SSIGN = 72, 73, 74, 75, 76, 77, 78
V_CW = 80
NV = V_CW + 8 * 31


class Buf:
    __slots__ = ("w", "r", "name")

    def __init__(self, name=""):
        self.w = None
        self.r = {}
        self.name = name


class Prog:
    ENG = ("pe", "act", "dve", "pool", "sp")

    def __init__(self, nc):
        self.nc = nc
        self.ops = {e: [] for e in self.ENG}
        self.sem = {e: nc.alloc_semaphore("s_" + e) for e in self.ENG if e != "sp"}
        self.cnt = {e: 0 for e in self.ENG}
        self.seen = {e: {} for e in self.ENG}
        self.dcnt = {}
        self.nsem = 0

    def newsem(self, name=None):
        self.nsem += 1
        s = self.nc.alloc_semaphore(name or ("d%d" % self.nsem))
        self.dcnt[s] = 0
        return s

    def _waits(self, eng, reads, writes):
        need = {}
        seq = len(self.ops[eng])

        def add(tok, raw):
            if tok is None:
                return
            sem, val, teng, tseq = tok
            if teng == eng:
                if eng == "pe" or eng == "sp":
                    return
                if not raw:
                    return
                if seq - tseq > 2:
                    return
            if need.get(sem, 0) < val:
                need[sem] = val

        for b in reads:
            add(b.w, True)
        for b in writes:
            add(b.w, False)
            for t in b.r.values():
                add(t, False)
        out = []
        for sem, val in need.items():
            if self.seen[eng].get(sem, 0) >= val:
                continue
            self.seen[eng][sem] = val
            out.append((sem, val))
        return out

    def op(self, eng, fn, reads=(), writes=(), signal=True):
        waits = self._waits(eng, reads, writes)
        seq = len(self.ops[eng])
        sem = self.sem[eng]
        if signal:
            self.cnt[eng] += 1
            tok = (sem, self.cnt[eng], eng, seq)
            sig = (sem, 1)
        else:
            tok = (sem, self.cnt[eng] + 1, eng, seq)
            sig = None
        self.ops[eng].append((waits, fn, sig))
        for b in reads:
            b.r[eng] = tok
        for b in writes:
            b.w = tok
            b.r = {}
        return tok

    def dma(self, q, out, in_, reads=(), writes=(), sem=None):
        waits = self._waits(q, reads, writes)
        self.dcnt[sem] += 16
        tok = (sem, self.dcnt[sem], "dma", None)
        self.ops[q].append((waits, lambda e, o=out, i=in_: e.dma_start(out=o, in_=i), (sem, 16)))
        for b in reads:
            b.r[sem] = tok
        for b in writes:
            b.w = tok
            b.r = {}
        return tok

    def custom(self, q, fn, reads=(), writes=(), sem=None, inc=1):
        waits = self._waits(q, reads, writes)
        self.dcnt[sem] += inc
        tok = (sem, self.dcnt[sem], "dma", None)
        self.ops[q].append((waits, fn, (sem, inc)))
        for b in reads:
            b.r[sem] = tok
        for b in writes:
            b.w = tok
            b.r = {}
        return tok

    def barrier(self):
        for e in self.ENG:
            waits = []
            for f in ("pe", "act", "dve", "pool"):
                if f != e and self.cnt[f] > self.seen[e].get(self.sem[f], 0):
                    waits.append((self.sem[f], self.cnt[f]))
                    self.seen[e][self.sem[f]] = self.cnt[f]
            for s, v in self.dcnt.items():
                if v > self.seen[e].get(s, 0):
                    waits.append((s, v))
                    self.seen[e][s] = v
            if waits:
                self.ops[e].append((waits, None, None))

    def check_deadlock(self):
        val = {}
        pc = {e: 0 for e in self.ENG}
        n = {e: len(self.ops[e]) for e in self.ENG}
        progress = True
        while progress:
            progress = False
            for e in self.ENG:
                while pc[e] < n[e]:
                    waits, fn, sig = self.ops[e][pc[e]]
                    if any(val.get(s, 0) < v for s, v in waits):
                        break
                    if sig is not None:
                        val[sig[0]] = val.get(sig[0], 0) + sig[1]
                    pc[e] += 1
                    progress = True
        stuck = {e: (pc[e], n[e]) for e in self.ENG if pc[e] < n[e]}
        return stuck

    def emit(self):
        nc = self.nc
        ops = self.ops
        with nc.Block() as block:
            def run(engine, lst):
                for waits, fn, sig in lst:
                    for sem, val in waits:
                        engine.wait_ge(sem, val)
                    if fn is None:
                        continue
                    ins = fn(engine)
                    if sig is not None:
                        ins.then_inc(sig[0], sig[1])

            @block.tensor
            def _(e):
                run(e, ops["pe"])

            @block.scalar
            def _(e):
                run(e, ops["act"])

            @block.vector
            def _(e):
                run(e, ops["dve"])

            @block.gpsimd
            def _(e):
                run(e, ops["pool"])

            @block.sync
            def _(e):
                run(e, ops["sp"])


def build_program(debug_stop=None, shard_w=False, debug_dump=False):
    nc = bass.Bass("TRN2", target_bir_lowering=False)
    P = Prog(nc)
    flags = set((debug_stop or "").split(","))
    if debug_stop is not None:
        debug_stop = "xT"

    def din(name, shape, dt):
        return nc.dram_tensor(name, shape, dt, kind="ExternalInput")

    x_d = din("x", [NTOK, D], F32)
    pos_d = din("pos", [1, NTOK], I32)
    cT_d = din("cT", [128, 16], F32)
    bW = Buf()
    wgather = []

    def dinw(name, shape):
        if shard_w == "none":
            return nc.dram_tensor(name + "_int", shape, F32)
        if not shard_w:
            return din(name, shape, F32)
        sh = din(name, [shape[0] // 8, shape[1]], F32)
        full = nc.dram_tensor(name + "_full", shape, F32)
        wgather.append((sh, full))
        return full

    wada_d = dinw("w_ada", [D, NADA])
    badaT_d = din("b_adaT", [128, 144], F32)
    wgu_d = [dinw("w_gu1", [D, 2 * FF]), dinw("w_gu2", [D, 2 * FF])]
    wdn_d = [dinw("w_down1", [FF, D]), dinw("w_down2", [FF, D])]
    win_d = dinw("w_in", [D, 5120])
    wout_d = dinw("w_out", [D, D])
    vecs_d = din("vecs", [128, NV], F32)
    lam_d = din("lam4", [1, 256], F32)
    masks_d = din("masks", [128, 8 * 512], BF16)
    cb16_d = din("cb16", [128, 4 * 128], BF16)
    identf_d = din("identf", [128, 128], F32)
    out_d = nc.dram_tensor("out", [NTOK, D], F32, kind="ExternalOutput")

    X1_d = nc.dram_tensor("X1", [NBLK * 128, 16 * 512], F32)
    QT_d = nc.dram_tensor("QT", [NBLK * 128, 8 * 2 * 512], BF16)
    UT_d = nc.dram_tensor("UT", [NBLK * 128, 8 * 512], BF16)
    KTo_d = [nc.dram_tensor("KTo%d" % i, [512, NTOK], BF16) for i in range(2)]
    KTa_d = [nc.dram_tensor("KTa%d" % i, [1024, NTOK], BF16) for i in range(2)]
    Vo_d = [nc.dram_tensor("Vo%d" % i, [NTOK // 2, 1024], BF16) for i in range(2)]
    Va_d = [nc.dram_tensor("Va%d" % i, [NTOK, 1024], BF16) for i in range(2)]
    Ho_d = nc.dram_tensor("Ho", [NBLK * 1024, 32], BF16)
    Ha_d = nc.dram_tensor("Ha", [2 * NBLK * 1024, 32], BF16)
    bX1 = [Buf() for _ in range(NBLK)]
    bQT = [Buf() for _ in range(NBLK)]
    bUT = [Buf() for _ in range(NBLK)]
    bKTo, bVo, bHo, bKTa, bVa, bHa = Buf(), Buf(), Buf(), Buf(), Buf(), Buf()

    def sb(name, shape, dt):
        return nc.alloc_sbuf_tensor("sb_" + name, shape, dt)

    xT = sb("xT", [128, 16, 512], F32)
    bxT = [Buf() for _ in range(16)]
    hT = sb("hT", [128, 16, 512], BF16)
    bhT = [Buf() for _ in range(16)]
    actT = sb("actT", [128, 8, 512], BF16)
    bact = [Buf() for _ in range(8)]
    wr = [sb("wr%d" % i, [128, 4096], BF16) for i in range(NSLOT)]
    bwr = [Buf() for _ in range(NSLOT)]
    swr = [P.newsem() for _ in range(NSLOT)]
    xs = [sb("xs%d" % i, [128, 2048], F32) for i in range(2)]
    bxs = [Buf() for _ in range(2)]
    sxs = [P.newsem() for _ in range(2)]
    NTMP = 6
    tmp = [sb("tmp%d" % i, [128, 512], F32) for i in range(NTMP)]
    btmp = [Buf() for _ in range(NTMP)]
    tmpi = [0]

    def newtmp():
        i = tmpi[0] % NTMP
        tmpi[0] += 1
        return tmp[i], btmp[i]

    sqb = [sb("sqb%d" % i, [128, 512], BF16) for i in range(2)]
    bsqb = [Buf() for _ in range(2)]
    sqi = [0]

    def newsq():
        i = sqi[0] % 2
        sqi[0] += 1
        return sqb[i], bsqb[i]

    qbt = [sb("qbt%d" % i, [128, 512], BF16) for i in range(2)]
    bqbt = [Buf() for _ in range(2)]
    qbi = [0]

    def newqb():
        i = qbi[0] % 2
        qbi[0] += 1
        return qbt[i], bqbt[i]

    identf = sb("identf", [128, 128], F32)
    onesf = sb("onesf", [128, 128], F32)
    cb16 = sb("cb16", [128, 4, 128], BF16)
    identb, onesb, blkones, pswap = cb16[:, 0, :], cb16[:, 1, :], cb16[:, 2, :], cb16[:, 3, :]
    vecs = sb("vecs", [128, NV], F32)
    bconst = Buf()
    adaT = sb("adaT", [128, 144], F32)
    badaT_s = sb("badaT", [128, 144], F32)
    mods = sb("mods", [128, 9 * 16], F32)
    bmods = Buf()
    cact = sb("cact", [128, 16], BF16)
    cTs = sb("cTs", [128, 16], F32)
    small = sb("small", [128, 16], F32)
    bsmall = Buf()
    lamrow = sb("lamrow", [1, 256], F32)
    lamtmp = sb("lamtmp", [1, 256], F32)
    lamcol = sb("lamcol", [128, 1], F32)
    negh = sb("negh", [128, 512], F32)
    bnegh = Buf()

    arena = sb("arena", [128, 8192], BF16)
    vst = arena[:, 0:4096].rearrange("p (a b) -> p a b", a=4)
    posi = arena[:, 4096:5120].bitcast(I32)
    cosT = arena[:, 5120:6144].bitcast(F32)
    sinT = arena[:, 6144:7168].bitcast(F32)
    kout = [arena[:, 7168:7680], arena[:, 7680:8192]]
    bposi = Buf()
    bcos, bsin = Buf(), Buf()
    qz = [sb("qz%d" % i, [128, 2, 512], BF16) for i in range(2)]
    bqz = [Buf() for _ in range(2)]
    bkout = [Buf() for _ in range(2)]
    bvst = [Buf() for _ in range(4)]
    uT = sb("uT", [128, 8, 544], BF16)
    buT = [Buf() for _ in range(8)]
    vh = [sb("vh%d" % i, [128, 32, 128], BF16) for i in range(2)]
    bvh = [Buf() for _ in range(2)]
    pt = [sb("pt%d" % i, [128, 512], BF16) for i in range(4)]
    bpt = [Buf() for _ in range(4)]
    masks = sb("masks", [128, 8, 512], BF16)
    diag = [arena[:, 0:3968].rearrange("p (a b) -> p a b", a=31), arena[:, 4096:4096 + 3968].rearrange("p (a b) -> p a b", a=31)]
    bdiag = [Buf() for _ in range(2)]
    convo = actT
    bconvo = bact
    hc = [sb("hc%d" % i, [128, 8, 32], BF16) for i in range(2)]
    bhc = [Buf() for _ in range(2)]
    hzero = sb("hzero", [128, 8, 32], BF16)
    kTh = [xs[i][:, :].bitcast(BF16).rearrange("p (r t) -> p r t", r=2) for i in range(2)]

    pbank = [nc.alloc_psum_tensor("pb%d" % i, [128, 512], F32) for i in range(8)]
    bpb = [Buf() for _ in range(8)]
    pbi = [0]

    def newps(lo=0, hi=8):
        n = hi - lo
        i = lo + pbi[0] % n
        pbi[0] += 1
        return pbank[i], bpb[i]

    sgen = P.newsem()
    sst = [P.newsem() for _ in range(4)]
    ssti = [0]

    def stsem():
        s = sst[ssti[0] % 4]
        ssti[0] += 1
        return s

    s_qzst = [P.newsem() for _ in range(2)]
    s_kost = [P.newsem() for _ in range(2)]
    s_vst = [P.newsem() for _ in range(4)]
    s_ut1, s_ut2, s_utl, s_hc0, s_hc1 = P.newsem(), P.newsem(), P.newsem(), P.newsem(), P.newsem()
    s_x1st, s_x1ld, s_posi = P.newsem(), P.newsem(), P.newsem()

    def mm(out, lhsT, rhs, start, stop, reads, wbuf, signal=None):
        P.op("pe", lambda e: e.matmul(out, lhsT=lhsT, rhs=rhs, start=start, stop=stop),
             reads=reads, writes=[wbuf], signal=(stop if signal is None else signal))

    def act(out, in_, func, reads, writes, bias=None, scale=None):
        kw = {}
        if bias is not None:
            kw["bias"] = bias
        if scale is not None:
            kw["scale"] = scale
        P.op("act", lambda e: e.activation(out=out, in_=in_, func=func, **kw), reads=reads, writes=writes)

    def tt(eng, out, in0, in1, op, reads, writes):
        P.op(eng, lambda e: e.tensor_tensor(out=out, in0=in0, in1=in1, op=op), reads=reads, writes=writes)

    def ts(eng, out, in0, s1, op0, reads, writes, s2=None, op1=None):
        if op1 is None and eng == "pool" and op0 == ALU.mult:
            P.op(eng, lambda e: e.tensor_scalar(out=out, in0=in0, scalar1=s1, scalar2=1.0, op0=ALU.mult, op1=ALU.mult),
                 reads=reads, writes=writes)
        elif op1 is None:
            P.op(eng, lambda e: e.tensor_scalar(out=out, in0=in0, scalar1=s1, scalar2=None, op0=op0),
                 reads=reads, writes=writes)
        else:
            P.op(eng, lambda e: e.tensor_scalar(out=out, in0=in0, scalar1=s1, scalar2=s2, op0=op0, op1=op1),
                 reads=reads, writes=writes)

    def stt(out, in0, scalar, in1, op0, op1, reads, writes):
        P.op("dve", lambda e: e.scalar_tensor_tensor(out=out, in0=in0, scalar=scalar, in1=in1, op0=op0, op1=op1),
             reads=reads, writes=writes)

    def cp(eng, out, in_, reads, writes):
        P.op(eng, lambda e: e.tensor_copy(out=out, in_=in_), reads=reads, writes=writes)

    epsc = sb("epsc", [128, 2], F32)
    bepsc = Buf()

    def eps_ap(eps):
        return epsc[:, 0:1] if eps == RMS_EPS else epsc[:, 1:2]

    rstd_p = sb("rstd_p", [128, 512], F32)
    brstd_p = Buf()
    mean_p = sb("mean_p", [128, 512], F32)
    bmean_p = Buf()

    def rsqrt_bcast(ps, bps, scale, eps, dst=None):
        t, bt = newtmp() if dst is None else dst
        act(t[:], ps[:], AF.Sqrt, [bps], [bt], scale=scale, bias=eps_ap(eps))
        P.op("dve", lambda e, t=t: e.reciprocal(out=t[:], in_=t[:]), reads=[bt], writes=[bt])
        return t, bt

    units = []

    def plan_units():
        wav = wada_d.ap().rearrange("(k p) n -> p k n", p=128)
        def ada_units(lo, hi):
            for g in range(lo, hi):
                units.append(("ada", wav[:, :, g * 256:(g + 1) * 256], 16, 256))

        ada_units(0, 40)

        def ffn(l):
            wg = wgu_d[l].ap().rearrange("(k p) n -> p k n", p=128)
            wd = wdn_d[l].ap().rearrange("(j p) n -> p j n", p=128)
            for grp in range(6):
                nu = 4 if grp < 5 else 2
                for u in range(nu):
                    c0 = (grp * 4 + u) * 256
                    units.append(("gu", wg[:, :, c0:c0 + 256], 16, 256))
                    units.append(("gu", wg[:, :, FF + c0:FF + c0 + 256], 16, 256))
                nf = nu * 2
                for cg in range(4):
                    units.append(("dn", wd[:, grp * 8:grp * 8 + nf, cg * 512:(cg + 1) * 512], nf, 512))

        wi = win_d.ap().rearrange("(k p) n -> p k n", p=128)
        wo = wout_d.ap().rearrange("(k p) n -> p k n", p=128)
        for j in range(NBLK):
            ffn(0)
            order = list(range(12))
            for a in range(4):
                order += [12 + a, 16 + a]
            for u in order:
                units.append(("in", wi[:, :, u * 256:(u + 1) * 256], 16, 256))
        ada_units(40, 72)
        for j in range(NBLK):
            for u in range(8):
                units.append(("out", wo[:, :, u * 256:(u + 1) * 256], 16, 256))
            ffn(1)

    plan_units()
    wstate = {"issued": 0, "next": 0}

    def wissue(upto):
        upto = min(upto, len(units))
        while wstate["issued"] < upto:
            i = wstate["issued"]
            tag, src, k, n = units[i]
            s = i % NSLOT
            dst = wr[s][:, 0:k * n].rearrange("p (k n) -> p k n", k=k)
            P.dma("pool", dst, src, reads=[bW], writes=[bwr[s]], sem=swr[s])
            wstate["issued"] += 1

    def wnext(tag):
        i = wstate["next"]
        assert units[i][0] == tag, (units[i][0], tag, i)
        wissue(i + NSLOT - 1)
        wstate["next"] += 1
        _, _, k, n = units[i]
        s = i % NSLOT
        return wr[s][:, 0:k * n].rearrange("p (k n) -> p k n", k=k), bwr[s]

    if shard_w is True:
        sgw = P.newsem()
        sgb = P.newsem()
        bWb = Buf()
        for sh, full in wgather:
            shi = nc.dram_tensor(sh.name + "_shi", list(sh.shape), F32)
            P.dma("sp", shi.ap(), sh.ap(), writes=[bWb], sem=sgb)
            P.custom("pool", lambda e, shi=shi, full=full: e.collective_compute(
                "AllGather", ALU.bypass, replica_groups=[list(range(8))], ins=[shi.ap()], outs=[full.ap()]),
                reads=[bWb], writes=[bW], sem=sgw)
    P.dma("sp", identf[:], identf_d.ap(), writes=[bconst], sem=sgen)
    P.dma("sp", cb16[:], cb16_d.ap().rearrange("p (a b) -> p a b", a=4), writes=[bconst], sem=sgen)
    P.dma("sp", vecs[:], vecs_d.ap(), writes=[bconst], sem=sgen)
    P.dma("sp", badaT_s[:], badaT_d.ap(), writes=[bconst], sem=sgen)
    P.dma("sp", cTs[:], cT_d.ap(), writes=[bconst], sem=sgen)
    P.dma("sp", lamrow[:], lam_d.ap(), writes=[bconst], sem=sgen)
    P.dma("sp", masks[:], masks_d.ap().rearrange("p (a b) -> p a b", a=8), writes=[bconst], sem=sgen)
    bconst.w = (sgen, P.dcnt[sgen], "dma", None)
    if debug_stop is not None:
        P.dma("sp", posi, pos_d.ap()[0:1, 0:512].broadcast_to([128, 512]), writes=[bposi], sem=s_posi)
    wissue(NSLOT)
    P.op("dve", lambda e: e.memset(negh[:], -0.5), writes=[bnegh])
    P.op("dve", lambda e: e.memset(epsc[:, 0:1], RMS_EPS), writes=[bepsc])
    P.op("dve", lambda e: e.memset(epsc[:, 1:2], LN_EPS), writes=[bepsc])
    P.op("dve", lambda e: e.memset(onesf[:], 1.0), writes=[bsmall])
    P.op("dve", lambda e: e.memset(hzero[:], 0.0), writes=[bsmall])
    P.op("dve", lambda e: e.memset(lamcol[:], 0.0), writes=[bsmall])
    for i in range(2):
        P.op("pool", lambda e, i=i: e.memset(qz[i][:], 0.0), writes=[bqz[i]])
    act(cact[:], cTs[:], AF.Silu, [bconst], [bsmall])
    if "nolam" in flags:
        P.barrier()
        P.emit()
        return nc
    tt("dve", lamtmp[:, 0:64], lamrow[:, 0:64], lamrow[:, 64:128], ALU.mult, [bconst], [bsmall])
    tt("dve", lamtmp[:, 64:128], lamrow[:, 128:192], lamrow[:, 192:256], ALU.mult, [bconst], [bsmall])
    P.op("dve", lambda e: e.reduce_sum(out=lamtmp[:, 128:130], in_=lamtmp[:, 0:128].rearrange("p (a b) -> p a b", a=2),
                                       axis=mybir.AxisListType.X), reads=[bsmall], writes=[bsmall])
    act(lamtmp[:, 130:132], lamtmp[:, 128:130], AF.Exp, [bsmall], [bsmall])
    ts("dve", lamtmp[:, 132:133], lamtmp[:, 131:132], lamtmp[:, 130:131], ALU.subtract, [bsmall], [bsmall],
       s2=-LAM_INIT, op1=ALU.add)
    cp("dve", lamcol[0:1, 0:1], lamtmp[:, 132:133], [bsmall], [bsmall])
    psb, bps_ = newps()
    P.op("pe", lambda e: e.matmul(psb[:, 0:1], lhsT=onesf[:], rhs=lamcol[:], start=True, stop=True),
         reads=[bsmall], writes=[bps_])
    cp("dve", small[:, 0:1], psb[:, 0:1], [bps_], [bsmall])
    ts("dve", small[:, 1:2], vecs[:, V_SUB:V_SUB + 1], 1.0 - LAM_INIT, ALU.mult, [bconst], [bsmall])

    def ada_part(g_lo, g_hi):
        psa, bpsa = newps()
        for g in range(g_lo, g_hi):
            w, bw = wnext("ada")
            for cc in range(2):
                col = g * 2 + cc
                for k in range(16):
                    P.op("pe", lambda e, w=w, k=k, cc=cc, col=col: e.matmul(
                        psa[:, col:col + 1], lhsT=w[:, k, cc * 128:(cc + 1) * 128], rhs=cact[:, k:k + 1],
                        start=(k == 0), stop=(k == 15)), reads=[bw, bsmall], writes=[bpsa], signal=(k == 15))
        c0, c1 = g_lo * 2, g_hi * 2
        tt("dve", adaT[:, c0:c1], psa[:, c0:c1], badaT_s[:, c0:c1], ALU.add, [bpsa, bconst], [bmods])

    def ada_ap(n):
        return adaT[:, n * 16:(n + 1) * 16]

    def mods_ap(n):
        return mods[:, n * 16:(n + 1) * 16]

    def mods_AB(si, vn):
        stt(mods_ap(3 * si), ada_ap(3 * si + 1), 1.0, vecs[:, vn:vn + 16], ALU.add, ALU.mult, [bmods, bconst], [bmods])
        cp("dve", mods_ap(3 * si + 1), ada_ap(3 * si), [bmods], [bmods])

    def mods_G(si, half):
        ts("dve", mods_ap(3 * si + 2), ada_ap(3 * si + 2), half, ALU.mult, [bmods], [bmods])

    if "noada" not in flags:
        ada_part(0, 40)
        mods_AB(0, V_N1)
        mods_G(0, 0.5)
        mods_AB(1, V_NM)

    def ada_units_small(g_lo, g_hi):
        for g in range(g_lo, g_hi):
            w, bw = wnext("ada")
            psx, bpsx = newps(0, 4)
            for cc in range(2):
                for k in range(16):
                    P.op("pe", lambda e, w=w, k=k, cc=cc, psx=psx: e.matmul(
                        psx[:, cc:cc + 1], lhsT=w[:, k, cc * 128:(cc + 1) * 128], rhs=cact[:, k:k + 1],
                        start=(k == 0), stop=(k == 15)), reads=[bw, bsmall], writes=[bpsx], signal=(k == 15))
            c0 = g * 2
            tt("dve", adaT[:, c0:c0 + 2], psx[:, 0:2], badaT_s[:, c0:c0 + 2], ALU.add, [bpsx, bconst], [bmods])

    def ada_tail_mods():
        mods_G(1, 1.0)
        mods_AB(2, V_N2)
        mods_G(2, 0.5)

    def modv(si, which, k):
        c = (3 * si + which) * 16 + k
        return mods[:, c:c + 1]

    xv = x_d.ap()
    ov = out_d.ap()
    xload_state = {"n": 0}

    def load_xtile(gt):
        s = gt % 2
        P.dma("sp", xs[s][:], xv[gt * 128:(gt + 1) * 128, :], writes=[bxs[s]], sem=sxs[s])

    def transpose_in(j):
        for t in range(4):
            gt = j * 4 + t
            s = gt % 2
            for g in range(4):
                ps, bps = newps()
                for dk in range(4):
                    k = 4 * g + dk
                    P.op("pe", lambda e, ps=ps, s=s, k=k, dk=dk: e.transpose(
                        ps[:, dk * 128:(dk + 1) * 128], xs[s][:, k * 128:(k + 1) * 128], identf[:]),
                        reads=[bxs[s], bconst], writes=[bps], signal=(dk == 3))
                act(xT[:, 4 * g:4 * g + 4, t * 128:(t + 1) * 128], ps[:].rearrange("p (a b) -> p a b", a=4),
                    AF.Copy, [bps], [bxT[4 * g + i] for i in range(4)])
            nxt = gt + 2
            if nxt < 16 and nxt < j * 4 + 6:
                load_xtile(nxt)

    def norm_mod(si):
        ps, bps = newps()
        for k in range(16):
            sq, bsq = newsq()
            tt("dve", sq[:], xT[:, k, :], xT[:, k, :], ALU.mult, [bxT[k]], [bsq])
            mm(ps[:], onesb, sq[:], k == 0, k == 15, [bsq, bconst], bps, signal=True)
        rstd, brs = rsqrt_bcast(ps, bps, 1.0 / D, RMS_EPS, dst=(rstd_p, brstd_p))
        for k in range(16):
            t, bt = newtmp()
            stt(t[:], xT[:, k, :], modv(si, 0, k), rstd[:], ALU.mult, ALU.mult, [bxT[k], bmods, brs], [bt])
            act(hT[:, k, :], t[:], AF.Identity, [bt, bmods], [bhT[k]], bias=modv(si, 1, k))

    def ffn(si):
        for grp in range(6):
            nu = 4 if grp < 5 else 2
            for u in range(nu):
                wg, bwg = wnext("gu")
                wu, bwu = wnext("gu")
                for cc in range(2):
                    fi = u * 2 + cc
                    psg, bpsg = newps()
                    psu, bpsu = newps()
                    for k in range(16):
                        mm(psg[:], wg[:, k, cc * 128:(cc + 1) * 128], hT[:, k, :], k == 0, k == 15, [bwg, bhT[k]], bpsg)
                    for k in range(16):
                        mm(psu[:], wu[:, k, cc * 128:(cc + 1) * 128], hT[:, k, :], k == 0, k == 15, [bwu, bhT[k]], bpsu)
                    t, bt = newtmp()
                    act(t[:], psg[:], AF.Silu, [bpsg], [bt])
                    tt("dve", actT[:, fi, :], t[:], psu[:], ALU.mult, [bt, bpsu], [bact[fi]])
            nf = nu * 2
            for cg in range(4):
                wd, bwd = wnext("dn")
                for dc in range(4):
                    k = cg * 4 + dc
                    ps, bps = newps()
                    for fi in range(nf):
                        mm(ps[:], wd[:, fi, dc * 128:(dc + 1) * 128], actT[:, fi, :], fi == 0, fi == nf - 1,
                           [bwd, bact[fi]], bps)
                    stt(xT[:, k, :], ps[:], modv(si, 2, k), xT[:, k, :], ALU.mult, ALU.add, [bps, bmods, bxT[k]], [bxT[k]])

    def rope_tables(j):
        P.dma("sp", posi, pos_d.ap()[0:1, j * 512:(j + 1) * 512].broadcast_to([128, 512]), writes=[bposi], sem=s_posi)
        ang, bang = newtmp()
        cp("dve", ang[:], posi, [bposi], [bang])
        ts("dve", ang[:], ang[:], vecs[:, V_INVF:V_INVF + 1], ALU.mult, [bang, bconst], [bang])
        kk, bkk = newtmp()
        ki, bki = newtmp()
        ts("dve", kk[:], ang[:], 1.0 / TWO_PI, ALU.mult, [bang], [bkk])
        kiv = ki[:].bitcast(I32)
        cp("dve", kiv, kk[:], [bkk], [bki])
        cp("dve", kk[:], kiv, [bki], [bkk])
        r, br = newtmp()
        stt(r[:], kk[:], -C1, ang[:], ALU.mult, ALU.add, [bkk, bang], [br])
        stt(r[:], kk[:], -C2, r[:], ALU.mult, ALU.add, [bkk, br], [br])
        m_, bm_ = newtmp()
        ts("dve", m_[:], r[:], math.pi, ALU.is_gt, [br], [bm_], s2=-TWO_PI, op1=ALU.mult)
        tt("dve", r[:], r[:], m_[:], ALU.add, [br, bm_], [br])
        ts("dve", m_[:], r[:], -math.pi, ALU.is_lt, [br], [bm_], s2=TWO_PI, op1=ALU.mult)
        tt("dve", r[:], r[:], m_[:], ALU.add, [br, bm_], [br])
        act(sinT, r[:], AF.Sin, [br], [bsin])
        ts("dve", sinT, sinT, vecs[:, V_SSIGN:V_SSIGN + 1], ALU.mult, [bsin, bconst], [bsin])
        c_, bc_ = newtmp()
        ts("dve", c_[:], r[:], math.pi / 2, ALU.add, [br], [bc_])
        ts("dve", m_[:], c_[:], math.pi, ALU.is_gt, [bc_], [bm_], s2=-TWO_PI, op1=ALU.mult)
        tt("dve", c_[:], c_[:], m_[:], ALU.add, [bc_, bm_], [bc_])
        act(cosT, c_[:], AF.Sin, [bc_], [bcos])

    def qk_chunk(ps, bps, gcol):
        qb, bqb = newqb()
        act(qb[:], ps[:], AF.Identity, [bps, bconst], [bqb], scale=vecs[:, gcol:gcol + 1])
        sq, bsq = newsq()
        act(sq[:], ps[:], AF.Square, [bps], [bsq])
        psA, bpsA = newps()
        mm(psA[:], blkones, sq[:], True, True, [bsq, bconst], bpsA)
        psB, bpsB = newps()
        mm(psB[:], pswap, qb[:], True, True, [bqb, bconst], bpsB)
        rstd, brs = rsqrt_bcast(psA, bpsA, 1.0 / 64, RMS_EPS)
        t1, bt1 = newtmp()
        tt("dve", t1[:], qb[:], cosT, ALU.mult, [bqb, bcos], [bt1])
        t2, bt2 = newtmp()
        tt("dve", t2[:], psB[:], sinT, ALU.mult, [bpsB, bsin], [bt2])
        tt("pool", t1[:], t1[:], t2[:], ALU.add, [bt1, bt2], [bt1])
        return t1, bt1, rstd, brs

    def w_in_block(j):
        for u in range(4):
            w, bw = wnext("in")
            for cc in range(2):
                hd = u * 2 + cc
                ps, bps = newps()
                for k in range(16):
                    mm(ps[:], w[:, k, cc * 128:(cc + 1) * 128], hT[:, k, :], k == 0, k == 15, [bw, bhT[k]], bps)
                t1, bt1, rstd, brs = qk_chunk(ps, bps, V_GQ)
                z = hd % 2
                tt("dve", qz[z][0:64, 0, :], t1[0:64, :], rstd[0:64, :], ALU.mult, [bt1, brs], [bqz[z]])
                tt("pool", qz[z][64:128, 1, :], t1[64:128, :], rstd[64:128, :], ALU.mult, [bt1, brs], [bqz[z]])
                P.dma("sp", QT_d.ap()[j * 128:(j + 1) * 128, hd * 1024:(hd + 1) * 1024],
                      qz[z][:].rearrange("p a b -> p (a b)"), reads=[bqz[z]], writes=[bQT[j]], sem=s_qzst[z])
        for u in range(4):
            w, bw = wnext("in")
            for cc in range(2):
                hd = u * 2 + cc
                ps, bps = newps()
                for k in range(16):
                    mm(ps[:], w[:, k, cc * 128:(cc + 1) * 128], hT[:, k, :], k == 0, k == 15, [bw, bhT[k]], bps)
                t1, bt1, rstd, brs = qk_chunk(ps, bps, V_GK)
                z = hd % 2
                tt("dve", kout[z], t1[:], rstd[:], ALU.mult, [bt1, brs], [bkout[z]])
                P.dma("sp", KTo_d[hd // 4].ap()[(hd % 4) * 128:(hd % 4 + 1) * 128, j * 512:(j + 1) * 512], kout[z],
                      reads=[bkout[z]], writes=[bKTo], sem=s_kost[z])
        for u in range(4):
            w, bw = wnext("in")
            for t in range(4):
                ps, bps = newps()
                for k in range(16):
                    mm(ps[:, 0:256], hT[:, k, t * 128:(t + 1) * 128], w[:, k, :], k == 0, k == 15, [bw, bhT[k]], bps)
                act(vst[:, t, u * 256:(u + 1) * 256], ps[:, 0:256], AF.Copy, [bps], [bvst[t]])
        for t in range(4):
            P.dma("sp", Vo_d[j // 2].ap()[(j % 2) * 512 + t * 128:(j % 2) * 512 + (t + 1) * 128, :], vst[:, t, :],
                  reads=[bvst[t]], writes=[bVo], sem=s_vst[t])
        for a in range(4):
            wa, bwa = wnext("in")
            wg, bwg = wnext("in")
            for cc in range(2):
                c = a * 2 + cc
                psa_, bpsa_ = newps()
                psg_, bpsg_ = newps()
                for k in range(16):
                    mm(psa_[:], wa[:, k, cc * 128:(cc + 1) * 128], hT[:, k, :], k == 0, k == 15, [bwa, bhT[k]], bpsa_)
                for k in range(16):
                    mm(psg_[:], wg[:, k, cc * 128:(cc + 1) * 128], hT[:, k, :], k == 0, k == 15, [bwg, bhT[k]], bpsg_)
                t, bt = newtmp()
                act(t[:], psg_[:], AF.Sigmoid, [bpsg_], [bt])
                tt("dve", uT[:, c, 32:544], t[:], psa_[:], ALU.mult, [bt, bpsa_], [buT[c]])
        P.dma("sp", UT_d.ap()[j * 128:(j + 1) * 128, :].rearrange("p (a b) -> p a b", a=8), uT[:, :, 32:544],
              reads=buT, writes=[bUT[j]], sem=s_ut1)
        P.dma("sp", Ho_d.ap()[j * 1024:(j + 1) * 1024, :].rearrange("(c p) t -> p c t", p=128), uT[:, :, 512:544],
              reads=buT, writes=[bHo], sem=s_ut2)

    load_xtile(0)
    load_xtile(1)
    for j in range(NBLK):
        transpose_in(j)
        if debug_stop == "xT":
            break
        norm_mod(0)
        ffn(0)
        P.dma("sp", X1_d.ap()[j * 128:(j + 1) * 128, :].rearrange("p (a b) -> p a b", a=16), xT[:],
              reads=bxT, writes=[bX1[j]], sem=s_x1st)
        norm_mod(1)
        rope_tables(j)
        w_in_block(j)

    RG = [[0, 1], [2, 3], [4, 5], [6, 7]]
    sc1, sc2, sc3 = P.newsem(), P.newsem(), P.newsem()
    if "noxchg" not in flags:
        P.custom("pool", lambda e: e.collective_compute("AllGather", ALU.bypass, replica_groups=RG,
                                                        ins=[Ho_d.ap()], outs=[Ha_d.ap()]),
                 reads=[bHo], writes=[bHa], sem=sc1)
        for i in range(2):
            P.custom("pool", lambda e, i=i: e.collective_compute("AllGather", ALU.bypass, replica_groups=RG,
                                                            ins=[KTo_d[i].ap()], outs=[KTa_d[i].ap()]),
                     reads=[bKTo], writes=[bKTa], sem=sc2)
            P.custom("pool", lambda e, i=i: e.collective_compute("AllGather", ALU.bypass, replica_groups=RG,
                                                            ins=[Vo_d[i].ap()], outs=[Va_d[i].ap()]),
                     reads=[bVo], writes=[bVa], sem=sc3)
    P.barrier()

    skt = [P.newsem() for _ in range(2)]
    svh2 = [[P.newsem() for _ in range(4)] for _ in range(2)]
    bvh4 = [[Buf() for _ in range(4)] for _ in range(2)]
    sqz = [P.newsem() for _ in range(2)]
    KTv = [KTa_d[i].ap().rearrange("(r q) t -> q r t", r=2) for i in range(2)]
    Vv = [Va_d[i].ap().rearrange("(r n p) d -> p r n d", r=2, p=128) for i in range(2)]
    Hv = Ha_d.ap().rearrange("(r j c p) t -> r j p c t", r=2, j=NBLK, p=128)

    def load_head(j, hd):
        z = hd % 2
        P.dma("sp", qz[z][:].rearrange("p a b -> p (a b)"), QT_d.ap()[j * 128:(j + 1) * 128, hd * 1024:(hd + 1) * 1024],
              reads=[bQT[j]], writes=[bqz[z]], sem=sqz[z])
        P.dma("sp", kTh[z], KTv[hd // 4][(hd % 4) * 128:(hd % 4 + 1) * 128, :, :], reads=[bKTa], writes=[bxs[z]], sem=skt[z])
        vhv = vh[z][:].rearrange("p (r n) d -> p r n d", r=2)
        for i in range(2):
            for r in range(2):
                P.dma("sp", vhv[:, r, i * 8:(i + 1) * 8, :], Vv[i][:, r, :, hd * 128:(hd + 1) * 128],
                      reads=[bVa], writes=[bvh4[z][i * 2 + r]], sem=svh2[z][i * 2 + r])

    def attention_block(j):
        load_head(j, 0)
        nkb = 8 * j + 8
        for hd in range(8):
            z = hd % 2
            if hd + 1 < 8:
                load_head(j, hd + 1)
            ulist = [(kb, m) for kb in range(nkb) for m in range(2)]
            Sps = {}

            def issue_S(ui):
                kb, m = ulist[ui]
                G = kb // 4
                r = G % 2
                lt = (G // 2) * 4 + kb % 4
                ps, bps = newps(0, 4)
                mm(ps[:], kTh[z][:, r, lt * 128:(lt + 1) * 128], qz[z][:, m, :], True, True, [bxs[z], bqz[z]], bps)
                Sps[ui] = (ps, bps)

            issue_S(0)
            issue_S(1)
            for ui, (kb, m) in enumerate(ulist):
                if ui + 2 < len(ulist):
                    issue_S(ui + 2)
                ps, bps = Sps.pop(ui)
                G = kb // 4
                r = G % 2
                lt = (G // 2) * 4 + kb % 4
                pi = ui % 4
                act(pt[pi][:], ps[:], AF.Exp, [bps], [bpt[pi]], scale=0.125)
                if kb >= 8 * j:
                    eng = "dve" if (ui % 2 == 0) else "pool"
                    tt(eng, pt[pi][:], pt[pi][:], masks[:, kb - 8 * j, :], ALU.mult, [bpt[pi], bconst], [bpt[pi]])
                mm(pbank[4 + 2 * m][:], vh[z][:, r * 16 + lt, :], pt[pi][:], kb == 0, kb == nkb - 1,
                   [bvh4[z][(lt // 8) * 2 + r], bpt[pi]], bpb[4 + 2 * m])
                mm(pbank[5 + 2 * m][:], onesb, pt[pi][:], kb == 0, kb == nkb - 1, [bconst, bpt[pi]], bpb[5 + 2 * m])
            r0, br0 = newtmp()
            P.op("dve", lambda e, r0=r0: e.reciprocal(out=r0[:], in_=pbank[5][:]), reads=[bpb[5]], writes=[br0])
            o, bo = newtmp()
            tt("dve", o[:], pbank[4][:], r0[:], ALU.mult, [bpb[4], br0], [bo])
            r1, br1 = newtmp()
            P.op("dve", lambda e, r1=r1: e.reciprocal(out=r1[:], in_=pbank[7][:]), reads=[bpb[7]], writes=[br1])
            o1, bo1 = newtmp()
            tt("dve", o1[:], pbank[6][:], r1[:], ALU.mult, [bpb[6], br1], [bo1])
            stt(o[:], o1[:], small[:, 0:1], o[:], ALU.mult, ALU.add, [bo1, bsmall, bo], [bo])
            sq, bsq = newsq()
            tt("pool", sq[:], o[:], o[:], ALU.mult, [bo], [bsq])
            psS, bpsS = newps(0, 4)
            mm(psS[:], onesb, sq[:], True, True, [bsq, bconst], bpsS)
            rstd, brs = rsqrt_bcast(psS, bpsS, 1.0 / 128, RMS_EPS)
            stt(hT[:, hd, :], o[:], small[:, 1:2], rstd[:], ALU.mult, ALU.mult, [bo, bsmall, brs], [bhT[hd]])
            if j == 0:
                ada_units_small(40 + hd * 4, 44 + hd * 4)
        if j == 0:
            ada_tail_mods()

    def conv_block(j):
        P.dma("sp", uT[:, :, 32:544], UT_d.ap()[j * 128:(j + 1) * 128, :].rearrange("p (a b) -> p a b", a=8),
              reads=[bUT[j]], writes=buT, sem=s_utl)
        P.dma("sp", hc[0][:], Hv[0, j], reads=[bHa], writes=[bhc[0]], sem=s_hc0)
        if j > 0:
            P.dma("sp", hc[1][:], Hv[1, j - 1], reads=[bHa], writes=[bhc[1]], sem=s_hc1)
            h1 = hc[1]
        else:
            h1 = hzero
        ht, bht = newtmp()
        htv = ht[:, 0:256].rearrange("p (a b) -> p a b", a=8)
        ts("dve", htv, hc[0][:], vecs[:, V_SEL0:V_SEL0 + 1], ALU.mult, [bhc[0], bconst], [bht])
        stt(uT[:, :, 0:32], h1[:], vecs[:, V_SEL1:V_SEL1 + 1], htv, ALU.mult, ALU.add, [bhc[1], bconst, bht, bsmall], buT)
        ps1, bps1 = newps(4, 6)
        ps2, bps2 = pbank[6], bpb[6]
        if ps1 is pbank[4]:
            ps2, bps2 = pbank[7], bpb[7]
        for c in range(8):
            dz = c % 2
            for k in range(31):
                ts("dve" if k % 2 == 0 else "pool", diag[dz][:, k, :], identb, vecs[:, V_CW + c * 31 + k:V_CW + c * 31 + k + 1],
                   ALU.mult, [bconst], [bdiag[dz]])
            ps, bps = newps(0, 4)
            for k in range(31):
                mm(ps[:], diag[dz][:, k, :], uT[:, c, 2 + k:2 + k + 512], k == 0, k == 30, [bdiag[dz], buT[c]], bps)
            act(convo[:, c, :], ps[:], AF.Identity, [bps, bconst], [bconvo[c]], bias=vecs[:, V_CB + c:V_CB + c + 1])
            sq, bsq = newsq()
            tt("dve", sq[:], convo[:, c, :], convo[:, c, :], ALU.mult, [bconvo[c]], [bsq])
            mm(ps1[:], onesb, convo[:, c, :], c == 0, c == 7, [bconvo[c], bconst], bps1, signal=True)
            mm(ps2[:], onesb, sq[:], c == 0, c == 7, [bsq, bconst], bps2, signal=True)
        mean, bmean = mean_p, bmean_p
        ts("dve", mean[:], ps1[:], 1.0 / 1024, ALU.mult, [bps1], [bmean])
        var, bvar = rstd_p, brstd_p
        ts("dve", var[:], ps2[:], 1.0 / 1024, ALU.mult, [bps2], [bvar])
        m2, bm2 = newtmp()
        tt("dve", m2[:], mean[:], mean[:], ALU.mult, [bmean], [bm2])
        tt("dve", var[:], var[:], m2[:], ALU.subtract, [bvar, bm2], [bvar])
        act(var[:], var[:], AF.Sqrt, [bvar], [bvar], bias=eps_ap(LN_EPS))
        P.op("dve", lambda e: e.reciprocal(out=var[:], in_=var[:]), reads=[bvar], writes=[bvar])
        for c in range(8):
            t, bt = newtmp()
            tt("dve" if c % 2 == 0 else "pool", t[:], convo[:, c, :], mean[:], ALU.subtract, [bconvo[c], bmean], [bt])
            tt("pool" if c % 2 == 0 else "dve", t[:], t[:], var[:], ALU.mult, [bt, bvar], [bt])
            act(hT[:, 8 + c, :], t[:], AF.Silu, [bt, bconst], [bhT[8 + c]],
                scale=vecs[:, V_LG + c:V_LG + c + 1], bias=vecs[:, V_LB + c:V_LB + c + 1])

    def w_out_block():
        for u in range(8):
            w, bw = wnext("out")
            for cc in range(2):
                k = u * 2 + cc
                ps, bps = newps()
                for ke in range(16):
                    mm(ps[:], w[:, ke, cc * 128:(cc + 1) * 128], hT[:, ke, :], ke == 0, ke == 15, [bw, bhT[ke]], bps)
                stt(xT[:, k, :], ps[:], modv(1, 2, k), xT[:, k, :], ALU.mult, ALU.add, [bps, bmods, bxT[k]], [bxT[k]])

    def transpose_out(j):
        for t in range(4):
            gt = j * 4 + t
            s = gt % 2
            for g in range(4):
                ps, bps = newps()
                for dk in range(4):
                    k = 4 * g + dk
                    P.op("pe", lambda e, ps=ps, k=k, dk=dk, t=t: e.transpose(
                        ps[:, dk * 128:(dk + 1) * 128], xT[:, k, t * 128:(t + 1) * 128], identf[:]),
                        reads=[bxT[k], bconst], writes=[bps], signal=(dk == 3))
                act(xs[s][:, g * 512:(g + 1) * 512], ps[:], AF.Copy, [bps], [bxs[s]])
            P.dma("sp", ov[gt * 128:(gt + 1) * 128, :], xs[s][:], reads=[bxs[s]], sem=sxs[s])

    if debug_stop is None:
        for j in range(NBLK):
            attention_block(j)
            conv_block(j)
            P.dma("sp", xT[:], X1_d.ap()[j * 128:(j + 1) * 128, :].rearrange("p (a b) -> p a b", a=16),
                  reads=[bX1[j]], writes=bxT, sem=s_x1ld)
            w_out_block()
            norm_mod(2)
            ffn(2)
            transpose_out(j)
    else:
        transpose_out(0)

    if debug_dump:
        sdd = P.newsem()
        P.barrier()
        for nm, t in (("X1", X1_d), ("QT", QT_d), ("UT", UT_d), ("KTa0", KTa_d[0]), ("KTa1", KTa_d[1]), ("Va0", Va_d[0]), ("Va1", Va_d[1]), ("Ha", Ha_d)):
            o = nc.dram_tensor("dbg_" + nm, list(t.shape), t.dtype, kind="ExternalOutput")
            P.dma("sp", o.ap(), t.ap(), sem=sdd)
    P.barrier()
    stuck = P.check_deadlock()
    assert not stuck, ("deadlock", stuck)
    P.emit()
    nc._prog_stats = {e: len(P.ops[e]) for e in P.ENG}
    return nc


def _host_prep(inputs, shard_w=False):
    f32 = np.float32
    x = np.asarray(inputs["x"], f32)
    c = np.asarray(inputs["c"], f32)
    pos = np.asarray(inputs["positions"], np.int32)

    def pk(v):
        return np.ascontiguousarray(np.asarray(v, f32).reshape(-1, 128).T)

    vecs = np.zeros((128, NV), f32)
    vecs[:, V_N1:V_N1 + 16] = pk(inputs["ffn1_norm"][0])
    vecs[:, V_NM:V_NM + 16] = pk(inputs["mix_norm"][0])
    vecs[:, V_N2:V_N2 + 16] = pk(inputs["ffn2_norm"][0])
    vecs[:, V_CB:V_CB + 8] = pk(inputs["conv_b"][0])
    vecs[:, V_LG:V_LG + 8] = pk(inputs["conv_ln_g"][0])
    vecs[:, V_LB:V_LB + 8] = pk(inputs["conv_ln_b"][0])
    vecs[:, V_GQ] = np.tile(np.asarray(inputs["q_norm"][0], f32), 2)
    vecs[:, V_GK] = np.tile(np.asarray(inputs["k_norm"][0], f32), 2)
    vecs[:, V_SUB] = np.asarray(inputs["subln"][0], f32)
    invf = (np.float32(10000.0) ** (-np.arange(0, 64, 2, dtype=f32) / np.float32(64))).astype(f32)
    vecs[:, V_INVF] = np.tile(invf, 4)
    vecs[:, V_SSIGN] = np.tile(np.concatenate([-np.ones(32, f32), np.ones(32, f32)]), 2)
    cw = np.asarray(inputs["conv_w"][0], f32)
    vecs[:, V_CW:V_CW + 248] = cw.reshape(31, 8, 128).transpose(2, 1, 0).reshape(128, 248)
    lam4 = np.concatenate([np.asarray(inputs[k][0], f32) for k in ("lambda_q1", "lambda_k1", "lambda_q2", "lambda_k2")])[None, :]
    b_adaT = np.ascontiguousarray(np.asarray(inputs["b_ada"][0], f32).reshape(144, 128).T)
    ident = np.eye(128, dtype=f32)
    ones = np.ones((128, 128), f32)
    blk = np.kron(np.eye(2, dtype=f32), np.ones((64, 64), f32))
    idx = np.arange(128)
    psw = np.zeros((128, 128), f32)
    psw[idx ^ 32, idx] = 1.0
    cb16 = np.stack([ident, ones, blk, psw], axis=1).reshape(128, 512).astype(ml_dtypes.bfloat16)
    kk = (np.arange(8)[None, :, None] * 128 + np.arange(128)[:, None, None])
    in_maps = []
    shared = {
        "w_ada": np.asarray(inputs["w_ada"][0], f32), "b_adaT": b_adaT,
        "w_gu1": np.asarray(inputs["ffn1_w_gu"][0], f32), "w_gu2": np.asarray(inputs["ffn2_w_gu"][0], f32),
        "w_down1": np.asarray(inputs["ffn1_w_down"][0], f32), "w_down2": np.asarray(inputs["ffn2_w_down"][0], f32),
        "w_in": np.asarray(inputs["w_in"][0], f32), "w_out": np.asarray(inputs["w_out"][0], f32),
        "lam4": lam4, "cb16": cb16, "identf": ident,
    }
    for core in range(8):
        b, h = core // 2, core % 2
        xl = np.ascontiguousarray(x[b].reshape(8, 512, D)[h::2].reshape(NTOK, D))
        pl = np.ascontiguousarray(pos[b].reshape(8, 512)[h::2].reshape(1, NTOK))
        v = vecs.copy()
        v[:, V_SEL0] = 1.0 if h == 1 else 0.0
        v[:, V_SEL1] = 1.0 if h == 0 else 0.0
        qq = h * 512 + np.arange(512)[None, None, :]
        mask = (kk <= qq).astype(f32).reshape(128, 8 * 512).astype(ml_dtypes.bfloat16)
        m = dict(shared)
        if shard_w == "none":
            for wn in ("w_ada", "w_gu1", "w_gu2", "w_down1", "w_down2", "w_in", "w_out"):
                del m[wn]
        elif shard_w:
            for wn in ("w_ada", "w_gu1", "w_gu2", "w_down1", "w_down2", "w_in", "w_out"):
                rr = shared[wn].shape[0] // 8
                m[wn] = np.ascontiguousarray(shared[wn][core * rr:(core + 1) * rr])
        m.update({"x": xl, "pos": pl, "cT": pk(c[b]), "vecs": v, "masks": mask})
        in_maps.append(m)
    return in_maps


_NC_CACHE = {}


def kernel(**inputs):
    import os
    dump = os.environ.get("MK_DUMP") == "1"
    in_maps = _host_prep(inputs)
    key = "nc_dump" if dump else "nc"
    if key not in _NC_CACHE:
        _NC_CACHE[key] = build_program(debug_dump=dump)
    nc = _NC_CACHE[key]
    res = run_bass_kernel_spmd(nc, in_maps, core_ids=list(range(8)))
    out = np.zeros((4, 4096, D), np.float32)
    for core in range(8):
        b, h = core // 2, core % 2
        y = np.asarray(res.results[core]["out"], np.float32).reshape(NBLK, 512, D)
        out[b].reshape(8, 512, D)[h::2] = y
    if dump:
        dd = os.environ.get("MK_DUMP_DIR", "/tmp")
        for core in range(2):
            for nm in ("X1", "QT", "UT", "KTa0", "KTa1", "Va0", "Va1", "Ha"):
                np.save(os.path.join(dd, "dbg_%s_%d.npy" % (nm, core)),
                        np.asarray(res.results[core]["dbg_" + nm]).astype(np.float32))
        np.save(os.path.join(dd, "out.npy"), out)
    return out
```

```python
import math
import numpy as np
import ml_dtypes
import concourse.bass as bass
import concourse.mybir as mybir
from concourse.bass_utils import run_bass_kernel_spmd

F32 = mybir.dt.float32
BF16 = mybir.dt.bfloat16
I32 = mybir.dt.int32
ALU = mybir.AluOpType
AF = mybir.ActivationFunctionType

D = 2048
FF = 5632
NBLK = 4
TB = 512
NTOK = NBLK * TB
NADA = 9 * D
RMS_EPS = 1e-6
LN_EPS = 1e-5
LAM_INIT = 0.8 - 0.6 * math.exp(-0.3 * 0)
NSLOT = 6
TWO_PI = 2.0 * math.pi
C1 = 6.28125
C2 = TWO_PI - C1

V_N1, V_NM, V_N2 = 0, 16, 32
V_CB, V_LG, V_LB = 48, 56, 64
V_GQ, V_GK, V_SUB, V_INVF, V_SEL0, V_SEL1, V_SSIGN = 72, 73, 74, 75, 76, 77, 78
V_CW = 80
NV = V_CW + 8 * 31


class Buf:
    __slots__ = ("w", "r", "name")

    def __init__(self, name=""):
        self.w = None
        self.r = {}
        self.name = name


class Prog:
    ENG = ("pe", "act", "dve", "pool", "sp")

    def __init__(self, nc):
        self.nc = nc
        self.ops = {e: [] for e in self.ENG}
        self.sem = {e: nc.alloc_semaphore("s_" + e) for e in self.ENG if e != "sp"}
        self.cnt = {e: 0 for e in self.ENG}
        self.seen = {e: {} for e in self.ENG}
        self.dcnt = {}
        self.nsem = 0

    def newsem(self, name=None):
        self.nsem += 1
        s = self.nc.alloc_semaphore(name or ("d%d" % self.nsem))
        self.dcnt[s] = 0
        return s

    def _waits(self, eng, reads, writes):
        need = {}
        seq = len(self.ops[eng])

        def add(tok, raw):
            if tok is None:
                return
            sem, val, teng, tseq = tok
            if teng == eng:
                if eng == "pe" or eng == "sp":
                    return
                if not raw:
                    return
                if seq - tseq > 2:
                    return
            if need.get(sem, 0) < val:
                need[sem] = val

        for b in reads:
            add(b.w, True)
        for b in writes:
            add(b.w, False)
            for t in b.r.values():
                add(t, False)
        out = []
        for sem, val in need.items():
            if self.seen[eng].get(sem, 0) >= val:
                continue
            self.seen[eng][sem] = val
            out.append((sem, val))
        return out

    def op(self, eng, fn, reads=(), writes=(), signal=True):
        waits = self._waits(eng, reads, writes)
        seq = len(self.ops[eng])
        sem = self.sem[eng]
        if signal:
            self.cnt[eng] += 1
            tok = (sem, self.cnt[eng], eng, seq)
            sig = (sem, 1)
        else:
            tok = (sem, self.cnt[eng] + 1, eng, seq)
            sig = None
        self.ops[eng].append((waits, fn, sig))
        for b in reads:
            b.r[eng] = tok
        for b in writes:
            b.w = tok
            b.r = {}
        return tok

    def dma(self, q, out, in_, reads=(), writes=(), sem=None):
        waits = self._waits(q, reads, writes)
        self.dcnt[sem] += 16
        tok = (sem, self.dcnt[sem], "dma", None)
        self.ops[q].append((waits, lambda e, o=out, i=in_: e.dma_start(out=o, in_=i), (sem, 16)))
        for b in reads:
            b.r[sem] = tok
        for b in writes:
            b.w = tok
            b.r = {}
        return tok

    def custom(self, q, fn, reads=(), writes=(), sem=None, inc=1):
        waits = self._waits(q, reads, writes)
        self.dcnt[sem] += inc
        tok = (sem, self.dcnt[sem], "dma", None)
        self.ops[q].append((waits, fn, (sem, inc)))
        for b in reads:
            b.r[sem] = tok
        for b in writes:
            b.w = tok
            b.r = {}
        return tok

    def barrier(self):
        for e in self.ENG:
            waits = []
            for f in ("pe", "act", "dve", "pool"):
                if f != e and self.cnt[f] > self.seen[e].get(self.sem[f], 0):
                    waits.append((self.sem[f], self.cnt[f]))
                    self.seen[e][self.sem[f]] = self.cnt[f]
            for s, v in self.dcnt.items():
                if v > self.seen[e].get(s, 0):
                    waits.append((s, v))
                    self.seen[e][s] = v
            if waits:
                self.ops[e].append((waits, None, None))

    def check_deadlock(self):
        val = {}
        pc = {e: 0 for e in self.ENG}
        n = {e: len(self.ops[e]) for e in self.ENG}
        progress = True
        while progress:
            progress = False
            for e in self.ENG:
                while pc[e] < n[e]:
                    waits, fn, sig = self.ops[e][pc[e]]
                    if any(val.get(s, 0) < v for s, v in waits):
                        break
                    if sig is not None:
                        val[sig[0]] = val.get(sig[0], 0) + sig[1]
                    pc[e] += 1
                    progress = True
        stuck = {e: (pc[e], n[e]) for e in self.ENG if pc[e] < n[e]}
        return stuck

    def emit(self):
        nc = self.nc
        ops = self.ops
        with nc.Block() as block:
            def run(engine, lst):
                for waits, fn, sig in lst:
                    for sem, val in waits:
                        engine.wait_ge(sem, val)
                    if fn is None:
                        continue
                    ins = fn(engine)
                    if sig is not None:
                        ins.then_inc(sig[0], sig[1])

            @block.tensor
            def _(e):
                run(e, ops["pe"])

            @block.scalar
            def _(e):
                run(e, ops["act"])

            @block.vector
            def _(e):
                run(e, ops["dve"])

            @block.gpsimd
            def _(e):
                run(e, ops["pool"])

            @block.sync
            def _(e):
                run(e, ops["sp"])


def build_program(debug_stop=None, shard_w=False, debug_dump=False):
    nc = bass.Bass("TRN2", target_bir_lowering=False)
    P = Prog(nc)
    flags = set((debug_stop or "").split(","))
    if debug_stop is not None:
        debug_stop = "xT"

    def din(name, shape, dt):
        return nc.dram_tensor(name, shape, dt, kind="ExternalInput")

    x_d = din("x", [NTOK, D], F32)
    pos_d = din("pos", [1, NTOK], I32)
    cT_d = din("cT", [128, 16], F32)
    bW = Buf()
    wgather = []

    def dinw(name, shape):
        if shard_w == "none":
            return nc.dram_tensor(name + "_int", shape, F32)
        if not shard_w:
            return din(name, shape, F32)
        sh = din(name, [shape[0] // 8, shape[1]], F32)
        full = nc.dram_tensor(name + "_full", shape, F32)
        wgather.append((sh, full))
        return full

    wada_d = dinw("w_ada", [D, NADA])
    badaT_d = din("b_adaT", [128, 144], F32)
    wgu_d = [dinw("w_gu1", [D, 2 * FF]), dinw("w_gu2", [D, 2 * FF])]
    wdn_d = [dinw("w_down1", [FF, D]), dinw("w_down2", [FF, D])]
    win_d = dinw("w_in", [D, 5120])
    wout_d = dinw("w_out", [D, D])
    vecs_d = din("vecs", [128, NV], F32)
    lam_d = din("lam4", [1, 256], F32)
    masks_d = din("masks", [128, 8 * 512], BF16)
    cb16_d = din("cb16", [128, 4 * 128], BF16)
    identf_d = din("identf", [128, 128], F32)
    out_d = nc.dram_tensor("out", [NTOK, D], F32, kind="ExternalOutput")

    X1_d = nc.dram_tensor("X1", [NBLK * 128, 16 * 512], F32)
    QT_d = nc.dram_tensor("QT", [NBLK * 128, 8 * 2 * 512], BF16)
    UT_d = nc.dram_tensor("UT", [NBLK * 128, 8 * 512], BF16)
    KTo_d = [nc.dram_tensor("KTo%d" % i, [512, NTOK], BF16) for i in range(2)]
    KTa_d = [nc.dram_tensor("KTa%d" % i, [1024, NTOK], BF16) for i in range(2)]
    Vo_d = [nc.dram_tensor("Vo%d" % i, [NTOK // 2, 1024], BF16) for i in range(2)]
    Va_d = [nc.dram_tensor("Va%d" % i, [NTOK, 1024], BF16) for i in range(2)]
    Ho_d = nc.dram_tensor("Ho", [NBLK * 1024, 32], BF16)
    Ha_d = nc.dram_tensor("Ha", [2 * NBLK * 1024, 32], BF16)
    bX1 = [Buf() for _ in range(NBLK)]
    bQT = [Buf() for _ in range(NBLK)]
    bUT = [Buf() for _ in range(NBLK)]
    bKTo, bVo, bHo, bKTa, bVa, bHa = Buf(), Buf(), Buf(), Buf(), Buf(), Buf()

    def sb(name, shape, dt):
        return nc.alloc_sbuf_tensor("sb_" + name, shape, dt)

    xT = sb("xT", [128, 16, 512], F32)
    bxT = [Buf() for _ in range(16)]
    hT = sb("hT", [128, 16, 512], BF16)
    bhT = [Buf() for _ in range(16)]
    actT = sb("actT", [128, 8, 512], BF16)
    bact = [Buf() for _ in range(8)]
    wr = [sb("wr%d" % i, [128, 4096], BF16) for i in range(NSLOT)]
    bwr = [Buf() for _ in range(NSLOT)]
    swr = [P.newsem() for _ in range(NSLOT)]
    xs = [sb("xs%d" % i, [128, 2048], F32) for i in range(2)]
    bxs = [Buf() for _ in range(2)]
    sxs = [P.newsem() for _ in range(2)]
    NTMP = 6
    tmp = [sb("tmp%d" % i, [128, 512], F32) for i in range(NTMP)]
    btmp = [Buf() for _ in range(NTMP)]
    tmpi = [0]

    def newtmp():
        i = tmpi[0] % NTMP
        tmpi[0] += 1
        return tmp[i], btmp[i]

    sqb = [sb("sqb%d" % i, [128, 512], BF16) for i in range(2)]
    bsqb = [Buf() for _ in range(2)]
    sqi = [0]

    def newsq():
        i = sqi[0] % 2
        sqi[0] += 1
        return sqb[i], bsqb[i]

    qbt = [sb("qbt%d" % i, [128, 512], BF16) for i in range(2)]
    bqbt = [Buf() for _ in range(2)]
    qbi = [0]

    def newqb():
        i = qbi[0] % 2
        qbi[0] += 1
        return qbt[i], bqbt[i]

    identf = sb("identf", [128, 128], F32)
    onesf = sb("onesf", [128, 128], F32)
    cb16 = sb("cb16", [128, 4, 128], BF16)
    identb, onesb, blkones, pswap = cb16[:, 0, :], cb16[:, 1, :], cb16[:, 2, :], cb16[:, 3, :]
    vecs = sb("vecs", [128, NV], F32)
    bconst = Buf()
    adaT = sb("adaT", [128, 144], F32)
    badaT_s = sb("badaT", [128, 144], F32)
    mods = sb("mods", [128, 9 * 16], F32)
    bmods = Buf()
    cact = sb("cact", [128, 16], BF16)
    cTs = sb("cTs", [128, 16], F32)
    small = sb("small", [128, 16], F32)
    bsmall = Buf()
    lamrow = sb("lamrow", [1, 256], F32)
    lamtmp = sb("lamtmp", [1, 256], F32)
    lamcol = sb("lamcol", [128, 1], F32)
    negh = sb("negh", [128, 512], F32)
    bnegh = Buf()

    arena = sb("arena", [128, 8192], BF16)
    vst = arena[:, 0:4096].rearrange("p (a b) -> p a b", a=4)
    posi = arena[:, 4096:5120].bitcast(I32)
    cosT = arena[:, 5120:6144].bitcast(F32)
    sinT = arena[:, 6144:7168].bitcast(F32)
    kout = [arena[:, 7168:7680], arena[:, 7680:8192]]
    bposi = Buf()
    bcos, bsin = Buf(), Buf()
    qz = [sb("qz%d" % i, [128, 2, 512], BF16) for i in range(2)]
    bqz = [Buf() for _ in range(2)]
    bkout = [Buf() for _ in range(2)]
    bvst = [Buf() for _ in range(4)]
    uT = sb("uT", [128, 8, 544], BF16)
    buT = [Buf() for _ in range(8)]
    vh = [sb("vh%d" % i, [128, 32, 128], BF16) for i in range(2)]
    bvh = [Buf() for _ in range(2)]
    pt = [sb("pt%d" % i, [128, 512], BF16) for i in range(4)]
    bpt = [Buf() for _ in range(4)]
    masks = sb("masks", [128, 8, 512], BF16)
    diag = [arena[:, 0:3968].rearrange("p (a b) -> p a b", a=31), arena[:, 4096:4096 + 3968].rearrange("p (a b) -> p a b", a=31)]
    bdiag = [Buf() for _ in range(2)]
    convo = actT
    bconvo = bact
    hc = [sb("hc%d" % i, [128, 8, 32], BF16) for i in range(2)]
    bhc = [Buf() for _ in range(2)]
    hzero = sb("hzero", [128, 8, 32], BF16)
    kTh = [xs[i][:, :].bitcast(BF16).rearrange("p (r t) -> p r t", r=2) for i in range(2)]

    pbank = [nc.alloc_psum_tensor("pb%d" % i, [128, 512], F32) for i in range(8)]
    bpb = [Buf() for _ in range(8)]
    pbi = [0]

    def newps(lo=0, hi=8):
        n = hi - lo
        i = lo + pbi[0] % n
        pbi[0] += 1
        return pbank[i], bpb[i]

    sgen = P.newsem()
    sst = [P.newsem() for _ in range(4)]
    ssti = [0]

    def stsem():
        s = sst[ssti[0] % 4]
        ssti[0] += 1
        return s

    s_qzst = [P.newsem() for _ in range(2)]
    s_kost = [P.newsem() for _ in range(2)]
    s_vst = [P.newsem() for _ in range(4)]
    s_ut1, s_ut2, s_utl, s_hc0, s_hc1 = P.newsem(), P.newsem(), P.newsem(), P.newsem(), P.newsem()
    s_x1st, s_x1ld, s_posi = P.newsem(), P.newsem(), P.newsem()

    def mm(out, lhsT, rhs, start, stop, reads, wbuf, signal=None):
        P.op("pe", lambda e: e.matmul(out, lhsT=lhsT, rhs=rhs, start=start, stop=stop),
             reads=reads, writes=[wbuf], signal=(stop if signal is None else signal))

    def act(out, in_, func, reads, writes, bias=None, scale=None):
        kw = {}
        if bias is not None:
            kw["bias"] = bias
        if scale is not None:
            kw["scale"] = scale
        P.op("act", lambda e: e.activation(out=out, in_=in_, func=func, **kw), reads=reads, writes=writes)

    def tt(eng, out, in0, in1, op, reads, writes):
        P.op(eng, lambda e: e.tensor_tensor(out=out, in0=in0, in1=in1, op=op), reads=reads, writes=writes)

    def ts(eng, out, in0, s1, op0, reads, writes, s2=None, op1=None):
        if op1 is None and eng == "pool" and op0 == ALU.mult:
            P.op(eng, lambda e: e.tensor_scalar(out=out, in0=in0, scalar1=s1, scalar2=1.0, op0=ALU.mult, op1=ALU.mult),
                 reads=reads, writes=writes)
        elif op1 is None:
            P.op(eng, lambda e: e.tensor_scalar(out=out, in0=in0, scalar1=s1, scalar2=None, op0=op0),
                 reads=reads, writes=writes)
        else:
            P.op(eng, lambda e: e.tensor_scalar(out=out, in0=in0, scalar1=s1, scalar2=s2, op0=op0, op1=op1),
                 reads=reads, writes=writes)

    def stt(out, in0, scalar, in1, op0, op1, reads, writes):
        P.op("dve", lambda e: e.scalar_tensor_tensor(out=out, in0=in0, scalar=scalar, in1=in1, op0=op0, op1=op1),
             reads=reads, writes=writes)

    def cp(eng, out, in_, reads, writes):
        P.op(eng, lambda e: e.tensor_copy(out=out, in_=in_), reads=reads, writes=writes)

    epsc = sb("epsc", [128, 2], F32)
    bepsc = Buf()

    def eps_ap(eps):
        return epsc[:, 0:1] if eps == RMS_EPS else epsc[:, 1:2]

    rstd_p = sb("rstd_p", [128, 512], F32)
    brstd_p = Buf()
    mean_p = sb("mean_p", [128, 512], F32)
    bmean_p = Buf()

    def rsqrt_bcast(ps, bps, scale, eps, dst=None):
        t, bt = newtmp() if dst is None else dst
        act(t[:], ps[:], AF.Sqrt, [bps], [bt], scale=scale, bias=eps_ap(eps))
        P.op("dve", lambda e, t=t: e.reciprocal(out=t[:], in_=t[:]), reads=[bt], writes=[bt])
        return t, bt

    units = []

    def plan_units():
        wav = wada_d.ap().rearrange("(k p) n -> p k n", p=128)
        def ada_units(lo, hi):
            for g in range(lo, hi):
                units.append(("ada", wav[:, :, g * 256:(g + 1) * 256], 16, 256))

        ada_units(0, 40)

        def ffn(l):
            wg = wgu_d[l].ap().rearrange("(k p) n -> p k n", p=128)
            wd = wdn_d[l].ap().rearrange("(j p) n -> p j n", p=128)
            for grp in range(6):
                nu = 4 if grp < 5 else 2
                for u in range(nu):
                    c0 = (grp * 4 + u) * 256
                    units.append(("gu", wg[:, :, c0:c0 + 256], 16, 256))
                    units.append(("gu", wg[:, :, FF + c0:FF + c0 + 256], 16, 256))
                nf = nu * 2
                for cg in range(4):
                    units.append(("dn", wd[:, grp * 8:grp * 8 + nf, cg * 512:(cg + 1) * 512], nf, 512))

        wi = win_d.ap().rearrange("(k p) n -> p k n", p=128)
        wo = wout_d.ap().rearrange("(k p) n -> p k n", p=128)
        for j in range(NBLK):
            ffn(0)
            order = list(range(12))
            for a in range(4):
                order += [12 + a, 16 + a]
            for u in order:
                units.append(("in", wi[:, :, u * 256:(u + 1) * 256], 16, 256))
        ada_units(40, 72)
        for j in range(NBLK):
            for u in range(8):
                units.append(("out", wo[:, :, u * 256:(u + 1) * 256], 16, 256))
            ffn(1)

    plan_units()
    wstate = {"issued": 0, "next": 0}

    def wissue(upto):
        upto = min(upto, len(units))
        while wstate["issued"] < upto:
            i = wstate["issued"]
            tag, src, k, n = units[i]
            s = i % NSLOT
            dst = wr[s][:, 0:k * n].rearrange("p (k n) -> p k n", k=k)
            P.dma("pool", dst, src, reads=[bW], writes=[bwr[s]], sem=swr[s])
            wstate["issued"] += 1

    def wnext(tag):
        i = wstate["next"]
        assert units[i][0] == tag, (units[i][0], tag, i)
        wissue(i + NSLOT - 1)
        wstate["next"] += 1
        _, _, k, n = units[i]
        s = i % NSLOT
        return wr[s][:, 0:k * n].rearrange("p (k n) -> p k n", k=k), bwr[s]

    if shard_w is True:
        sgw = P.newsem()
        sgb = P.newsem()
        bWb = Buf()
        for sh, full in wgather:
            shi = nc.dram_tensor(sh.name + "_shi", list(sh.shape), F32)
            P.dma("sp", shi.ap(), sh.ap(), writes=[bWb], sem=sgb)
            P.custom("pool", lambda e, shi=shi, full=full: e.collective_compute(
                "AllGather", ALU.bypass, replica_groups=[list(range(8))], ins=[shi.ap()], outs=[full.ap()]),
                reads=[bWb], writes=[bW], sem=sgw)
    P.dma("sp", identf[:], identf_d.ap(), writes=[bconst], sem=sgen)
    P.dma("sp", cb16[:], cb16_d.ap().rearrange("p (a b) -> p a b", a=4), writes=[bconst], sem=sgen)
    P.dma("sp", vecs[:], vecs_d.ap(), writes=[bconst], sem=sgen)
    P.dma("sp", badaT_s[:], badaT_d.ap(), writes=[bconst], sem=sgen)
    P.dma("sp", cTs[:], cT_d.ap(), writes=[bconst], sem=sgen)
    P.dma("sp", lamrow[:], lam_d.ap(), writes=[bconst], sem=sgen)
    P.dma("sp", masks[:], masks_d.ap().rearrange("p (a b) -> p a b", a=8), writes=[bconst], sem=sgen)
    bconst.w = (sgen, P.dcnt[sgen], "dma", None)
    if debug_stop is not None:
        P.dma("sp", posi, pos_d.ap()[0:1, 0:512].broadcast_to([128, 512]), writes=[bposi], sem=s_posi)
    wissue(NSLOT)
    P.op("dve", lambda e: e.memset(negh[:], -0.5), writes=[bnegh])
    P.op("dve", lambda e: e.memset(epsc[:, 0:1], RMS_EPS), writes=[bepsc])
    P.op("dve", lambda e: e.memset(epsc[:, 1:2], LN_EPS), writes=[bepsc])
    P.op("dve", lambda e: e.memset(onesf[:], 1.0), writes=[bsmall])
    P.op("dve", lambda e: e.memset(hzero[:], 0.0), writes=[bsmall])
    P.op("dve", lambda e: e.memset(lamcol[:], 0.0), writes=[bsmall])
    for i in range(2):
        P.op("pool", lambda e, i=i: e.memset(qz[i][:], 0.0), writes=[bqz[i]])
    act(cact[:], cTs[:], AF.Silu, [bconst], [bsmall])
    if "nolam" in flags:
        P.barrier()
        P.emit()
        return nc
    tt("dve", lamtmp[:, 0:64], lamrow[:, 0:64], lamrow[:, 64:128], ALU.mult, [bconst], [bsmall])
    tt("dve", lamtmp[:, 64:128], lamrow[:, 128:192], lamrow[:, 192:256], ALU.mult, [bconst], [bsmall])
    P.op("dve", lambda e: e.reduce_sum(out=lamtmp[:, 128:130], in_=lamtmp[:, 0:128].rearrange("p (a b) -> p a b", a=2),
                                       axis=mybir.AxisListType.X), reads=[bsmall], writes=[bsmall])
    act(lamtmp[:, 130:132], lamtmp[:, 128:130], AF.Exp, [bsmall], [bsmall])
    ts("dve", lamtmp[:, 132:133], lamtmp[:, 131:132], lamtmp[:, 130:131], ALU.subtract, [bsmall], [bsmall],
       s2=-LAM_INIT, op1=ALU.add)
    cp("dve", lamcol[0:1, 0:1], lamtmp[:, 132:133], [bsmall], [bsmall])
    psb, bps_ = newps()
    P.op("pe", lambda e: e.matmul(psb[:, 0:1], lhsT=onesf[:], rhs=lamcol[:], start=True, stop=True),
         reads=[bsmall], writes=[bps_])
    cp("dve", small[:, 0:1], psb[:, 0:1], [bps_], [bsmall])
    ts("dve", small[:, 1:2], vecs[:, V_SUB:V_SUB + 1], 1.0 - LAM_INIT, ALU.mult, [bconst], [bsmall])

    def ada_part(g_lo, g_hi):
        psa, bpsa = newps()
        for g in range(g_lo, g_hi):
            w, bw = wnext("ada")
            for cc in range(2):
                col = g * 2 + cc
                for k in range(16):
                    P.op("pe", lambda e, w=w, k=k, cc=cc, col=col: e.matmul(
                        psa[:, col:col + 1], lhsT=w[:, k, cc * 128:(cc + 1) * 128], rhs=cact[:, k:k + 1],
                        start=(k == 0), stop=(k == 15)), reads=[bw, bsmall], writes=[bpsa], signal=(k == 15))
        c0, c1 = g_lo * 2, g_hi * 2
        tt("dve", adaT[:, c0:c1], psa[:, c0:c1], badaT_s[:, c0:c1], ALU.add, [bpsa, bconst], [bmods])

    def ada_ap(n):
        return adaT[:, n * 16:(n + 1) * 16]

    def mods_ap(n):
        return mods[:, n * 16:(n + 1) * 16]

    def mods_AB(si, vn):
        stt(mods_ap(3 * si), ada_ap(3 * si + 1), 1.0, vecs[:, vn:vn + 16], ALU.add, ALU.mult, [bmods, bconst], [bmods])
        cp("dve", mods_ap(3 * si + 1), ada_ap(3 * si), [bmods], [bmods])

    def mods_G(si, half):
        ts("dve", mods_ap(3 * si + 2), ada_ap(3 * si + 2), half, ALU.mult, [bmods], [bmods])

    if "noada" not in flags:
        ada_part(0, 40)
        mods_AB(0, V_N1)
        mods_G(0, 0.5)
        mods_AB(1, V_NM)

    def ada_units_small(g_lo, g_hi):
        for g in range(g_lo, g_hi):
            w, bw = wnext("ada")
            psx, bpsx = newps(0, 4)
            for cc in range(2):
                for k in range(16):
                    P.op("pe", lambda e, w=w, k=k, cc=cc, psx=psx: e.matmul(
                        psx[:, cc:cc + 1], lhsT=w[:, k, cc * 128:(cc + 1) * 128], rhs=cact[:, k:k + 1],
                        start=(k == 0), stop=(k == 15)), reads=[bw, bsmall], writes=[bpsx], signal=(k == 15))
            c0 = g * 2
            tt("dve", adaT[:, c0:c0 + 2], psx[:, 0:2], badaT_s[:, c0:c0 + 2], ALU.add, [bpsx, bconst], [bmods])

    def ada_tail_mods():
        mods_G(1, 1.0)
        mods_AB(2, V_N2)
        mods_G(2, 0.5)

    def modv(si, which, k):
        c = (3 * si + which) * 16 + k
        return mods[:, c:c + 1]

    xv = x_d.ap()
    ov = out_d.ap()
    xload_state = {"n": 0}

    def load_xtile(gt):
        s = gt % 2
        P.dma("sp", xs[s][:], xv[gt * 128:(gt + 1) * 128, :], writes=[bxs[s]], sem=sxs[s])

    def transpose_in(j):
        for t in range(4):
            gt = j * 4 + t
            s = gt % 2
            for g in range(4):
                ps, bps = newps()
                for dk in range(4):
                    k = 4 * g + dk
                    P.op("pe", lambda e, ps=ps, s=s, k=k, dk=dk: e.transpose(
                        ps[:, dk * 128:(dk + 1) * 128], xs[s][:, k * 128:(k + 1) * 128], identf[:]),
                        reads=[bxs[s], bconst], writes=[bps], signal=(dk == 3))
                act(xT[:, 4 * g:4 * g + 4, t * 128:(t + 1) * 128], ps[:].rearrange("p (a b) -> p a b", a=4),
                    AF.Copy, [bps], [bxT[4 * g + i] for i in range(4)])
            nxt = gt + 2
            if nxt < 16 and nxt < j * 4 + 6:
                load_xtile(nxt)

    def norm_mod(si):
        ps, bps = newps()
        for k in range(16):
            sq, bsq = newsq()
            tt("dve", sq[:], xT[:, k, :], xT[:, k, :], ALU.mult, [bxT[k]], [bsq])
            mm(ps[:], onesb, sq[:], k == 0, k == 15, [bsq, bconst], bps, signal=True)
        rstd, brs = rsqrt_bcast(ps, bps, 1.0 / D, RMS_EPS, dst=(rstd_p, brstd_p))
        for k in range(16):
            t, bt = newtmp()
            stt(t[:], xT[:, k, :], modv(si, 0, k), rstd[:], ALU.mult, ALU.mult, [bxT[k], bmods, brs], [bt])
            act(hT[:, k, :], t[:], AF.Identity, [bt, bmods], [bhT[k]], bias=modv(si, 1, k))

    def ffn(si):
        for grp in range(6):
            nu = 4 if grp < 5 else 2
            for u in range(nu):
                wg, bwg = wnext("gu")
                wu, bwu = wnext("gu")
                for cc in range(2):
                    fi = u * 2 + cc
                    psg, bpsg = newps()
                    psu, bpsu = newps()
                    for k in range(16):
                        mm(psg[:], wg[:, k, cc * 128:(cc + 1) * 128], hT[:, k, :], k == 0, k == 15, [bwg, bhT[k]], bpsg)
                    for k in range(16):
                        mm(psu[:], wu[:, k, cc * 128:(cc + 1) * 128], hT[:, k, :], k == 0, k == 15, [bwu, bhT[k]], bpsu)
                    t, bt = newtmp()
                    act(t[:], psg[:], AF.Silu, [bpsg], [bt])
                    tt("dve", actT[:, fi, :], t[:], psu[:], ALU.mult, [bt, bpsu], [bact[fi]])
            nf = nu * 2
            for cg in range(4):
                wd, bwd = wnext("dn")
                for dc in range(4):
                    k = cg * 4 + dc
                    ps, bps = newps()
                    for fi in range(nf):
                        mm(ps[:], wd[:, fi, dc * 128:(dc + 1) * 128], actT[:, fi, :], fi == 0, fi == nf - 1,
                           [bwd, bact[fi]], bps)
                    stt(xT[:, k, :], ps[:], modv(si, 2, k), xT[:, k, :], ALU.mult, ALU.add, [bps, bmods, bxT[k]], [bxT[k]])

    def rope_tables(j):
        P.dma("sp", posi, pos_d.ap()[0:1, j * 512:(j + 1) * 512].broadcast_to([128, 512]), writes=[bposi], sem=s_posi)
        ang, bang = newtmp()
        cp("dve", ang[:], posi, [bposi], [bang])
        ts("dve", ang[:], ang[:], vecs[:, V_INVF:V_INVF + 1], ALU.mult, [bang, bconst], [bang])
        kk, bkk = newtmp()
        ki, bki = newtmp()
        ts("dve", kk[:], ang[:], 1.0 / TWO_PI, ALU.mult, [bang], [bkk])
        kiv = ki[:].bitcast(I32)
        cp("dve", kiv, kk[:], [bkk], [bki])
        cp("dve", kk[:], kiv, [bki], [bkk])
        r, br = newtmp()
        stt(r[:], kk[:], -C1, ang[:], ALU.mult, ALU.add, [bkk, bang], [br])
        stt(r[:], kk[:], -C2, r[:], ALU.mult, ALU.add, [bkk, br], [br])
        m_, bm_ = newtmp()
        ts("dve", m_[:], r[:], math.pi, ALU.is_gt, [br], [bm_], s2=-TWO_PI, op1=ALU.mult)
        tt("dve", r[:], r[:], m_[:], ALU.add, [br, bm_], [br])
        ts("dve", m_[:], r[:], -math.pi, ALU.is_lt, [br], [bm_], s2=TWO_PI, op1=ALU.mult)
        tt("dve", r[:], r[:], m_[:], ALU.add, [br, bm_], [br])
        act(sinT, r[:], AF.Sin, [br], [bsin])
        ts("dve", sinT, sinT, vecs[:, V_SSIGN:V_SSIGN + 1], ALU.mult, [bsin, bconst], [bsin])
        c_, bc_ = newtmp()
        ts("dve", c_[:], r[:], math.pi / 2, ALU.add, [br], [bc_])
        ts("dve", m_[:], c_[:], math.pi, ALU.is_gt, [bc_], [bm_], s2=-TWO_PI, op1=ALU.mult)
        tt("dve", c_[:], c_[:], m_[:], ALU.add, [bc_, bm_], [bc_])
        act(cosT, c_[:], AF.Sin, [bc_], [bcos])

    def qk_chunk(ps, bps, gcol):
        qb, bqb = newqb()
        act(qb[:], ps[:], AF.Identity, [bps, bconst], [bqb], scale=vecs[:, gcol:gcol + 1])
        sq, bsq = newsq()
        act(sq[:], ps[:], AF.Square, [bps], [bsq])
        psA, bpsA = newps()
        mm(psA[:], blkones, sq[:], True, True, [bsq, bconst], bpsA)
        psB, bpsB = newps()
        mm(psB[:], pswap, qb[:], True, True, [bqb, bconst], bpsB)
        rstd, brs = rsqrt_bcast(psA, bpsA, 1.0 / 64, RMS_EPS)
        t1, bt1 = newtmp()
        tt("dve", t1[:], qb[:], cosT, ALU.mult, [bqb, bcos], [bt1])
        t2, bt2 = newtmp()
        tt("dve", t2[:], psB[:], sinT, ALU.mult, [bpsB, bsin], [bt2])
        tt("pool", t1[:], t1[:], t2[:], ALU.add, [bt1, bt2], [bt1])
        return t1, bt1, rstd, brs

    def w_in_block(j):
        for u in range(4):
            w, bw = wnext("in")
            for cc in range(2):
                hd = u * 2 + cc
                ps, bps = newps()
                for k in range(16):
                    mm(ps[:], w[:, k, cc * 128:(cc + 1) * 128], hT[:, k, :], k == 0, k == 15, [bw, bhT[k]], bps)
                t1, bt1, rstd, brs = qk_chunk(ps, bps, V_GQ)
                z = hd % 2
                tt("dve", qz[z][0:64, 0, :], t1[0:64, :], rstd[0:64, :], ALU.mult, [bt1, brs], [bqz[z]])
                tt("pool", qz[z][64:128, 1, :], t1[64:128, :], rstd[64:128, :], ALU.mult, [bt1, brs], [bqz[z]])
                P.dma("sp", QT_d.ap()[j * 128:(j + 1) * 128, hd * 1024:(hd + 1) * 1024],
                      qz[z][:].rearrange("p a b -> p (a b)"), reads=[bqz[z]], writes=[bQT[j]], sem=s_qzst[z])
        for u in range(4):
            w, bw = wnext("in")
            for cc in range(2):
                hd = u * 2 + cc
                ps, bps = newps()
                for k in range(16):
                    mm(ps[:], w[:, k, cc * 128:(cc + 1) * 128], hT[:, k, :], k == 0, k == 15, [bw, bhT[k]], bps)
                t1, bt1, rstd, brs = qk_chunk(ps, bps, V_GK)
                z = hd % 2
                tt("dve", kout[z], t1[:], rstd[:], ALU.mult, [bt1, brs], [bkout[z]])
                P.dma("sp", KTo_d[hd // 4].ap()[(hd % 4) * 128:(hd % 4 + 1) * 128, j * 512:(j + 1) * 512], kout[z],
                      reads=[bkout[z]], writes=[bKTo], sem=s_kost[z])
        for u in range(4):
            w, bw = wnext("in")
            for t in range(4):
                ps, bps = newps()
                for k in range(16):
                    mm(ps[:, 0:256], hT[:, k, t * 128:(t + 1) * 128], w[:, k, :], k == 0, k == 15, [bw, bhT[k]], bps)
                act(vst[:, t, u * 256:(u + 1) * 256], ps[:, 0:256], AF.Copy, [bps], [bvst[t]])
        for t in range(4):
            P.dma("sp", Vo_d[j // 2].ap()[(j % 2) * 512 + t * 128:(j % 2) * 512 + (t + 1) * 128, :], vst[:, t, :],
                  reads=[bvst[t]], writes=[bVo], sem=s_vst[t])
        for a in range(4):
            wa, bwa = wnext("in")
            wg, bwg = wnext("in")
            for cc in range(2):
                c = a * 2 + cc
                psa_, bpsa_ = newps()
                psg_, bpsg_ = newps()
                for k in range(16):
                    mm(psa_[:], wa[:, k, cc * 128:(cc + 1) * 128], hT[:, k, :], k == 0, k == 15, [bwa, bhT[k]], bpsa_)
                for k in range(16):
                    mm(psg_[:], wg[:, k, cc * 128:(cc + 1) * 128], hT[:, k, :], k == 0, k == 15, [bwg, bhT[k]], bpsg_)
                t, bt = newtmp()
                act(t[:], psg_[:], AF.Sigmoid, [bpsg_], [bt])
                tt("dve", uT[:, c, 32:544], t[:], psa_[:], ALU.mult, [bt, bpsa_], [buT[c]])
        P.dma("sp", UT_d.ap()[j * 128:(j + 1) * 128, :].rearrange("p (a b) -> p a b", a=8), uT[:, :, 32:544],
              reads=buT, writes=[bUT[j]], sem=s_ut1)
        P.dma("sp", Ho_d.ap()[j * 1024:(j + 1) * 1024, :].rearrange("(c p) t -> p c t", p=128), uT[:, :, 512:544],
              reads=buT, writes=[bHo], sem=s_ut2)

    load_xtile(0)
    load_xtile(1)
    for j in range(NBLK):
        transpose_in(j)
        if debug_stop == "xT":
            break
        norm_mod(0)
        ffn(0)
        P.dma("sp", X1_d.ap()[j * 128:(j + 1) * 128, :].rearrange("p (a b) -> p a b", a=16), xT[:],
              reads=bxT, writes=[bX1[j]], sem=s_x1st)
        norm_mod(1)
        rope_tables(j)
        w_in_block(j)

    RG = [[0, 1], [2, 3], [4, 5], [6, 7]]
    sc1, sc2, sc3 = P.newsem(), P.newsem(), P.newsem()
    if "noxchg" not in flags:
        P.custom("pool", lambda e: e.collective_compute("AllGather", ALU.bypass, replica_groups=RG,
                                                        ins=[Ho_d.ap()], outs=[Ha_d.ap()]),
                 reads=[bHo], writes=[bHa], sem=sc1)
        for i in range(2):
            P.custom("pool", lambda e, i=i: e.collective_compute("AllGather", ALU.bypass, replica_groups=RG,
                                                            ins=[KTo_d[i].ap()], outs=[KTa_d[i].ap()]),
                     reads=[bKTo], writes=[bKTa], sem=sc2)
            P.custom("pool", lambda e, i=i: e.collective_compute("AllGather", ALU.bypass, replica_groups=RG,
                                                            ins=[Vo_d[i].ap()], outs=[Va_d[i].ap()]),
                     reads=[bVo], writes=[bVa], sem=sc3)
    P.barrier()

    skt = [P.newsem() for _ in range(2)]
    svh2 = [[P.newsem() for _ in range(4)] for _ in range(2)]
    bvh4 = [[Buf() for _ in range(4)] for _ in range(2)]
    sqz = [P.newsem() for _ in range(2)]
    KTv = [KTa_d[i].ap().rearrange("(r q) t -> q r t", r=2) for i in range(2)]
    Vv = [Va_d[i].ap().rearrange("(r n p) d -> p r n d", r=2, p=128) for i in range(2)]
    Hv = Ha_d.ap().rearrange("(r j c p) t -> r j p c t", r=2, j=NBLK, p=128)

    def load_head(j, hd):
        z = hd % 2
        P.dma("sp", qz[z][:].rearrange("p a b -> p (a b)"), QT_d.ap()[j * 128:(j + 1) * 128, hd * 1024:(hd + 1) * 1024],
              reads=[bQT[j]], writes=[bqz[z]], sem=sqz[z])
        P.dma("sp", kTh[z], KTv[hd // 4][(hd % 4) * 128:(hd % 4 + 1) * 128, :, :], reads=[bKTa], writes=[bxs[z]], sem=skt[z])
        vhv = vh[z][:].rearrange("p (r n) d -> p r n d", r=2)
        for i in range(2):
            for r in range(2):
                P.dma("sp", vhv[:, r, i * 8:(i + 1) * 8, :], Vv[i][:, r, :, hd * 128:(hd + 1) * 128],
                      reads=[bVa], writes=[bvh4[z][i * 2 + r]], sem=svh2[z][i * 2 + r])

    def attention_block(j):
        load_head(j, 0)
        nkb = 8 * j + 8
        for hd in range(8):
            z = hd % 2
            if hd + 1 < 8:
                load_head(j, hd + 1)
            ulist = [(kb, m) for kb in range(nkb) for m in range(2)]
            Sps = {}

            def issue_S(ui):
                kb, m = ulist[ui]
                G = kb // 4
                r = G % 2
                lt = (G // 2) * 4 + kb % 4
                ps, bps = newps(0, 4)
                mm(ps[:], kTh[z][:, r, lt * 128:(lt + 1) * 128], qz[z][:, m, :], True, True, [bxs[z], bqz[z]], bps)
                Sps[ui] = (ps, bps)

            issue_S(0)
            issue_S(1)
            for ui, (kb, m) in enumerate(ulist):
                if ui + 2 < len(ulist):
                    issue_S(ui + 2)
                ps, bps = Sps.pop(ui)
                G = kb // 4
                r = G % 2
                lt = (G // 2) * 4 + kb % 4
                pi = ui % 4
                act(pt[pi][:], ps[:], AF.Exp, [bps], [bpt[pi]], scale=0.125)
                if kb >= 8 * j:
                    eng = "dve"
                    tt(eng, pt[pi][:], pt[pi][:], masks[:, kb - 8 * j, :], ALU.mult, [bpt[pi], bconst], [bpt[pi]])
                mm(pbank[4 + 2 * m][:], vh[z][:, r * 16 + lt, :], pt[pi][:], kb == 0, kb == nkb - 1,
                   [bvh4[z][(lt // 8) * 2 + r], bpt[pi]], bpb[4 + 2 * m])
                mm(pbank[5 + 2 * m][:], onesb, pt[pi][:], kb == 0, kb == nkb - 1, [bconst, bpt[pi]], bpb[5 + 2 * m])
            r0, br0 = newtmp()
            P.op("dve", lambda e, r0=r0: e.reciprocal(out=r0[:], in_=pbank[5][:]), reads=[bpb[5]], writes=[br0])
            o, bo = newtmp()
            tt("dve", o[:], pbank[4][:], r0[:], ALU.mult, [bpb[4], br0], [bo])
            r1, br1 = newtmp()
            P.op("dve", lambda e, r1=r1: e.reciprocal(out=r1[:], in_=pbank[7][:]), reads=[bpb[7]], writes=[br1])
            o1, bo1 = newtmp()
            tt("dve", o1[:], pbank[6][:], r1[:], ALU.mult, [bpb[6], br1], [bo1])
            stt(o[:], o1[:], small[:, 0:1], o[:], ALU.mult, ALU.add, [bo1, bsmall, bo], [bo])
            sq, bsq = newsq()
            tt("pool", sq[:], o[:], o[:], ALU.mult, [bo], [bsq])
            psS, bpsS = newps(0, 4)
            mm(psS[:], onesb, sq[:], True, True, [bsq, bconst], bpsS)
            rstd, brs = rsqrt_bcast(psS, bpsS, 1.0 / 128, RMS_EPS)
            stt(hT[:, hd, :], o[:], small[:, 1:2], rstd[:], ALU.mult, ALU.mult, [bo, bsmall, brs], [bhT[hd]])
            if j == 0:
                ada_units_small(40 + hd * 4, 44 + hd * 4)
        if j == 0:
            ada_tail_mods()

    def conv_block(j):
        P.dma("sp", uT[:, :, 32:544], UT_d.ap()[j * 128:(j + 1) * 128, :].rearrange("p (a b) -> p a b", a=8),
              reads=[bUT[j]], writes=buT, sem=s_utl)
        P.dma("sp", hc[0][:], Hv[0, j], reads=[bHa], writes=[bhc[0]], sem=s_hc0)
        if j > 0:
            P.dma("sp", hc[1][:], Hv[1, j - 1], reads=[bHa], writes=[bhc[1]], sem=s_hc1)
            h1 = hc[1]
        else:
            h1 = hzero
        ht, bht = newtmp()
        htv = ht[:, 0:256].rearrange("p (a b) -> p a b", a=8)
        ts("dve", htv, hc[0][:], vecs[:, V_SEL0:V_SEL0 + 1], ALU.mult, [bhc[0], bconst], [bht])
        stt(uT[:, :, 0:32], h1[:], vecs[:, V_SEL1:V_SEL1 + 1], htv, ALU.mult, ALU.add, [bhc[1], bconst, bht, bsmall], buT)
        ps1, bps1 = newps(4, 6)
        ps2, bps2 = pbank[6], bpb[6]
        if ps1 is pbank[4]:
            ps2, bps2 = pbank[7], bpb[7]
        for c in range(8):
            dz = c % 2
            for k in range(31):
                ts("dve" if k % 2 == 0 else "pool", diag[dz][:, k, :], identb, vecs[:, V_CW + c * 31 + k:V_CW + c * 31 + k + 1],
                   ALU.mult, [bconst], [bdiag[dz]])
            ps, bps = newps(0, 4)
            for k in range(31):
                mm(ps[:], diag[dz][:, k, :], uT[:, c, 2 + k:2 + k + 512], k == 0, k == 30, [bdiag[dz], buT[c]], bps)
            act(convo[:, c, :], ps[:], AF.Identity, [bps, bconst], [bconvo[c]], bias=vecs[:, V_CB + c:V_CB + c + 1])
            sq, bsq = newsq()
            tt("dve", sq[:], convo[:, c, :], convo[:, c, :], ALU.mult, [bconvo[c]], [bsq])
            mm(ps1[:], onesb, convo[:, c, :], c == 0, c == 7, [bconvo[c], bconst], bps1, signal=True)
            mm(ps2[:], onesb, sq[:], c == 0, c == 7, [bsq, bconst], bps2, signal=True)
        mean, bmean = mean_p, bmean_p
        ts("dve", mean[:], ps1[:], 1.0 / 1024, ALU.mult, [bps1], [bmean])
        var, bvar = rstd_p, brstd_p
        ts("dve", var[:], ps2[:], 1.0 / 1024, ALU.mult, [bps2], [bvar])
        m2, bm2 = newtmp()
        tt("dve", m2[:], mean[:], mean[:], ALU.mult, [bmean], [bm2])
        tt("dve", var[:], var[:], m2[:], ALU.subtract, [bvar, bm2], [bvar])
        act(var[:], var[:], AF.Sqrt, [bvar], [bvar], bias=eps_ap(LN_EPS))
        P.op("dve", lambda e: e.reciprocal(out=var[:], in_=var[:]), reads=[bvar], writes=[bvar])
        for c in range(8):
            t, bt = newtmp()
            tt("dve" if c % 2 == 0 else "pool", t[:], convo[:, c, :], mean[:], ALU.subtract, [bconvo[c], bmean], [bt])
            tt("pool" if c % 2 == 0 else "dve", t[:], t[:], var[:], ALU.mult, [bt, bvar], [bt])
            act(hT[:, 8 + c, :], t[:], AF.Silu, [bt, bconst], [bhT[8 + c]],
                scale=vecs[:, V_LG + c:V_LG + c + 1], bias=vecs[:, V_LB + c:V_LB + c + 1])

    def w_out_block():
        for u in range(8):
            w, bw = wnext("out")
            for cc in range(2):
                k = u * 2 + cc
                ps, bps = newps()
                for ke in range(16):
                    mm(ps[:], w[:, ke, cc * 128:(cc + 1) * 128], hT[:, ke, :], ke == 0, ke == 15, [bw, bhT[ke]], bps)
                stt(xT[:, k, :], ps[:], modv(1, 2, k), xT[:, k, :], ALU.mult, ALU.add, [bps, bmods, bxT[k]], [bxT[k]])

    def transpose_out(j):
        for t in range(4):
            gt = j * 4 + t
            s = gt % 2
            for g in range(4):
                ps, bps = newps()
                for dk in range(4):
                    k = 4 * g + dk
                    P.op("pe", lambda e, ps=ps, k=k, dk=dk, t=t: e.transpose(
                        ps[:, dk * 128:(dk + 1) * 128], xT[:, k, t * 128:(t + 1) * 128], identf[:]),
                        reads=[bxT[k], bconst], writes=[bps], signal=(dk == 3))
                act(xs[s][:, g * 512:(g + 1) * 512], ps[:], AF.Copy, [bps], [bxs[s]])
            P.dma("sp", ov[gt * 128:(gt + 1) * 128, :], xs[s][:], reads=[bxs[s]], sem=sxs[s])

    if debug_stop is None:
        for j in range(NBLK):
            attention_block(j)
            conv_block(j)
            P.dma("sp", xT[:], X1_d.ap()[j * 128:(j + 1) * 128, :].rearrange("p (a b) -> p a b", a=16),
                  reads=[bX1[j]], writes=bxT, sem=s_x1ld)
            w_out_block()
            norm_mod(2)
            ffn(2)
            transpose_out(j)
    else:
        transpose_out(0)

    if debug_dump:
        sdd = P.newsem()
        P.barrier()
        for nm, t in (("X1", X1_d), ("QT", QT_d), ("UT", UT_d), ("KTa0", KTa_d[0]), ("KTa1", KTa_d[1]), ("Va0", Va_d[0]), ("Va1", Va_d[1]), ("Ha", Ha_d)):
            o = nc.dram_tensor("dbg_" + nm, list(t.shape), t.dtype, kind="ExternalOutput")
            P.dma("sp", o.ap(), t.ap(), sem=sdd)
    P.barrier()
    stuck = P.check_deadlock()
    assert not stuck, ("deadlock", stuck)
    P.emit()
    nc._prog_stats = {e: len(P.ops[e]) for e in P.ENG}
    return nc


def _host_prep(inputs, shard_w=False):
    f32 = np.float32
    x = np.asarray(inputs["x"], f32)
    c = np.asarray(inputs["c"], f32)
    pos = np.asarray(inputs["positions"], np.int32)

    def pk(v):
        return np.ascontiguousarray(np.asarray(v, f32).reshape(-1, 128).T)

    vecs = np.zeros((128, NV), f32)
    vecs[:, V_N1:V_N1 + 16] = pk(inputs["ffn1_norm"][0])
    vecs[:, V_NM:V_NM + 16] = pk(inputs["mix_norm"][0])
    vecs[:, V_N2:V_N2 + 16] = pk(inputs["ffn2_norm"][0])
    vecs[:, V_CB:V_CB + 8] = pk(inputs["conv_b"][0])
    vecs[:, V_LG:V_LG + 8] = pk(inputs["conv_ln_g"][0])
    vecs[:, V_LB:V_LB + 8] = pk(inputs["conv_ln_b"][0])
    vecs[:, V_GQ] = np.tile(np.asarray(inputs["q_norm"][0], f32), 2)
    vecs[:, V_GK] = np.tile(np.asarray(inputs["k_norm"][0], f32), 2)
    vecs[:, V_SUB] = np.asarray(inputs["subln"][0], f32)
    invf = (np.float32(10000.0) ** (-np.arange(0, 64, 2, dtype=f32) / np.float32(64))).astype(f32)
    vecs[:, V_INVF] = np.tile(invf, 4)
    vecs[:, V_SSIGN] = np.tile(np.concatenate([-np.ones(32, f32), np.ones(32, f32)]), 2)
    cw = np.asarray(inputs["conv_w"][0], f32)
    vecs[:, V_CW:V_CW + 248] = cw.reshape(31, 8, 128).transpose(2, 1, 0).reshape(128, 248)
    lam4 = np.concatenate([np.asarray(inputs[k][0], f32) for k in ("lambda_q1", "lambda_k1", "lambda_q2", "lambda_k2")])[None, :]
    b_adaT = np.ascontiguousarray(np.asarray(inputs["b_ada"][0], f32).reshape(144, 128).T)
    ident = np.eye(128, dtype=f32)
    ones = np.ones((128, 128), f32)
    blk = np.kron(np.eye(2, dtype=f32), np.ones((64, 64), f32))
    idx = np.arange(128)
    psw = np.zeros((128, 128), f32)
    psw[idx ^ 32, idx] = 1.0
    cb16 = np.stack([ident, ones, blk, psw], axis=1).reshape(128, 512).astype(ml_dtypes.bfloat16)
    kk = (np.arange(8)[None, :, None] * 128 + np.arange(128)[:, None, None])
    in_maps = []
    shared = {
        "w_ada": np.asarray(inputs["w_ada"][0], f32), "b_adaT": b_adaT,
        "w_gu1": np.asarray(inputs["ffn1_w_gu"][0], f32), "w_gu2": np.asarray(inputs["ffn2_w_gu"][0], f32),
        "w_down1": np.asarray(inputs["ffn1_w_down"][0], f32), "w_down2": np.asarray(inputs["ffn2_w_down"][0], f32),
        "w_in": np.asarray(inputs["w_in"][0], f32), "w_out": np.asarray(inputs["w_out"][0], f32),
        "lam4": lam4, "cb16": cb16, "identf": ident,
    }
    for core in range(8):
        b, h = core // 2, core % 2
        xl = np.ascontiguousarray(x[b].reshape(8, 512, D)[h::2].reshape(NTOK, D))
        pl = np.ascontiguousarray(pos[b].reshape(8, 512)[h::2].reshape(1, NTOK))
        v = vecs.copy()
        v[:, V_SEL0] = 1.0 if h == 1 else 0.0
        v[:, V_SEL1] = 1.0 if h == 0 else 0.0
        qq = h * 512 + np.arange(512)[None, None, :]
        mask = (kk <= qq).astype(f32).reshape(128, 8 * 512).astype(ml_dtypes.bfloat16)
        m = dict(shared)
        if shard_w == "none":
            for wn in ("w_ada", "w_gu1", "w_gu2", "w_down1", "w_down2", "w_in", "w_out"):
                del m[wn]
        elif shard_w:
            for wn in ("w_ada", "w_gu1", "w_gu2", "w_down1", "w_down2", "w_in", "w_out"):
                rr = shared[wn].shape[0] // 8
                m[wn] = np.ascontiguousarray(shared[wn][core * rr:(core + 1) * rr])
        m.update({"x": xl, "pos": pl, "cT": pk(c[b]), "vecs": v, "masks": mask})
        in_maps.append(m)
    return in_maps


_NC_CACHE = {}


def kernel(**inputs):
    import os
    dump = os.environ.get("MK_DUMP") == "1"
    in_maps = _host_prep(inputs)
    key = "nc_dump" if dump else "nc"
    if key not in _NC_CACHE:
        _NC_CACHE[key] = build_program(debug_dump=dump)
    nc = _NC_CACHE[key]
    res = run_bass_kernel_spmd(nc, in_maps, core_ids=list(range(8)))
    out = np.zeros((4, 4096, D), np.float32)
    for core in range(8):
        b, h = core // 2, core % 2
        y = np.asarray(res.results[core]["out"], np.float32).reshape(NBLK, 512, D)
        out[b].reshape(8, 512, D)[h::2] = y
    if dump:
        dd = os.environ.get("MK_DUMP_DIR", "/tmp")
        for core in range(2):
            for nm in ("X1", "QT", "UT", "KTa0", "KTa1", "Va0", "Va1", "Ha"):
                np.save(os.path.join(dd, "dbg_%s_%d.npy" % (nm, core)),
                        np.asarray(res.results[core]["dbg_" + nm]).astype(np.float32))
        np.save(os.path.join(dd, "out.npy"), out)
    return out
```
